# Optimizing a Trainium2 kernel written in Bass

```python
import math
import jax
import jax.numpy as jnp
from jax import lax
import numpy as np

D_MODEL = 1024
BATCH = 4
SEQ = 8192
DEPTH = 4

N_EVEN = (DEPTH + 1) // 2
N_ODD = DEPTH // 2
RMS_EPS = 1e-6
L2_EPS = 1e-6
D_FF = ((8 * D_MODEL + 3 * 256 - 1) // (3 * 256)) * 256

A_WIDTH = 256
A_CONV = 3
GDN_HEADS = 6
GDN_HEAD_DIM = 128
GDN_WIDTH = GDN_HEADS * GDN_HEAD_DIM
GDN_CONV = 4
GDN_CHUNK = 64
EV_SPLITS = (A_WIDTH, A_WIDTH, A_WIDTH, 3 * GDN_WIDTH, GDN_WIDTH, GDN_HEADS, GDN_HEADS)
EV_IN_COLS = sum(EV_SPLITS)
EV_MIX_WIDTH = A_WIDTH + GDN_WIDTH

RWKV_HEADS = 8
RWKV_HEAD_DIM = 64
RWKV_WIDTH = RWKV_HEADS * RWKV_HEAD_DIM
RWKV_W_LORA = 64
RWKV_A_LORA = 64
RWKV_V_LORA = 32
RWKV_G_LORA = 128
RWKV_LN_EPS = 64e-5
RWKV_SPLITS = (RWKV_WIDTH, RWKV_WIDTH, RWKV_WIDTH, RWKV_W_LORA, RWKV_A_LORA, RWKV_G_LORA)
RWKV_SHIFT_COLS = sum(RWKV_SPLITS)
MLA_HEADS = 8
MLA_NOPE = 64
MLA_ROPE = 32
MLA_V = 64
MLA_QK_DIM = MLA_NOPE + MLA_ROPE
MLA_Q_LORA = 512
MLA_KV_LORA = 256
MLA_WIDTH = MLA_HEADS * MLA_V
MLA_SPLITS = (MLA_Q_LORA, MLA_KV_LORA, MLA_ROPE)
MLA_IN_COLS = sum(MLA_SPLITS)
OD_IN_COLS = RWKV_SHIFT_COLS + MLA_IN_COLS
OD_MIX_WIDTH = RWKV_WIDTH + MLA_WIDTH
ROPE_THETA = 10000.0
Q_BLOCK = 128

kernel_name = 'hybrid_conv_gdn_rwkv7_mla_trunk'


def split_cols(z, sizes):
    return jnp.split(z, [int(s) for s in np.cumsum(sizes)[:-1]], axis=-1)


def rms_norm(x, gain, eps=RMS_EPS):
    xf = x.astype(jnp.float32)
    y = xf * lax.rsqrt(jnp.mean(xf * xf, axis=-1, keepdims=True) + eps)
    return (y * gain.astype(jnp.float32)).astype(x.dtype)


def l2_normalize(x):
    xf = x.astype(jnp.float32)
    return xf * lax.rsqrt(jnp.sum(xf * xf, axis=-1, keepdims=True) + L2_EPS)


def token_shift(z):
    return jnp.pad(z, ((0, 0), (1, 0), (0, 0)))[:, :-1]


def shift_lerp(z, mu):
    return z + mu * (token_shift(z) - z)


def causal_depthwise_conv(x, w):
    K = w.shape[0]
    T = x.shape[1]
    xp = jnp.pad(x, ((0, 0), (K - 1, 0), (0, 0)))
    y = xp[:, 0:T] * w[0]
    for j in range(1, K):
        y = y + xp[:, j:j + T] * w[j]
    return y


def swiglu(xn, w_gu, w_down):
    gate, up = jnp.split(xn @ w_gu, 2, axis=-1)
    return (jax.nn.silu(gate) * up) @ w_down


def rope_tables(positions):
    inv_freq = ROPE_THETA ** (-jnp.arange(0, MLA_ROPE, 2, dtype=jnp.float32) / MLA_ROPE)
    ang = positions.astype(jnp.float32)[..., None] * inv_freq
    return jnp.cos(ang), jnp.sin(ang)


def apply_rope(x, cos, sin):
    x = x.astype(jnp.float32)
    x1, x2 = jnp.split(x, 2, axis=-1)
    c = cos[:, :, None, :]
    s = sin[:, :, None, :]
    return jnp.concatenate([x1 * c - x2 * s, x2 * c + x1 * s], axis=-1)


def unit_lower_inverse(L):
    C = L.shape[-1]
    cols = jnp.arange(C)

    def body(i, A):
        row = A[..., i, :]
        upd = row + jnp.einsum('...j,...jk->...k', row, A)
        return A.at[..., i, :].set(jnp.where(cols < i, upd, row))

    A = lax.fori_loop(1, C, body, -L)
    return A + jnp.eye(C, dtype=L.dtype)


def to_chunks(t):
    b, T, h = t.shape[:3]
    t = t.reshape(b, T // GDN_CHUNK, GDN_CHUNK, h, *t.shape[3:])
    return jnp.moveaxis(t, 3, 1)


def gated_delta_rule_chunked(q, k, v, g, beta):
    B_, T, H, dk = q.shape
    dv = v.shape[-1]
    q = to_chunks(q * dk ** -0.5)
    k = to_chunks(k)
    v = to_chunks(v)
    beta = to_chunks(beta)
    g = jnp.cumsum(to_chunks(g), axis=-1)
    idx = jnp.arange(GDN_CHUNK)
    causal = idx[:, None] >= idx[None, :]
    strict = idx[:, None] > idx[None, :]
    diff = g[..., :, None] - g[..., None, :]
    decay = jnp.where(causal, jnp.exp(jnp.where(causal, diff, 0.0)), 0.0)
    kb = k * beta[..., None]
    L = jnp.where(strict, jnp.einsum('bhncd,bhnsd->bhncs', kb, k) * decay, 0.0)
    Tm = unit_lower_inverse(L)
    u = jnp.einsum('bhncs,bhnse->bhnce', Tm, v * beta[..., None])
    w = jnp.einsum('bhncs,bhnsd->bhncd', Tm, kb * jnp.exp(g)[..., None])
    qk = jnp.einsum('bhncd,bhnsd->bhncs', q, k) * decay
    q_dec = q * jnp.exp(g)[..., None]
    k_dec = k * jnp.exp(g[..., -1:] - g)[..., None]
    g_tot = jnp.exp(g[..., -1])
    xs = tuple(jnp.moveaxis(t, 2, 0) for t in (u, w, q_dec, k_dec, qk, g_tot))

    def step(S, inp):
        u_n, w_n, qd_n, kd_n, qk_n, gt_n = inp
        v_new = u_n - jnp.einsum('bhcd,bhde->bhce', w_n, S)
        o_n = jnp.einsum('bhcd,bhde->bhce', qd_n, S) + jnp.einsum('bhcs,bhse->bhce', qk_n, v_new)
        S = S * gt_n[..., None, None] + jnp.einsum('bhcd,bhce->bhde', kd_n, v_new)
        return S, o_n

    S0 = jnp.zeros((B_, H, dk, dv), jnp.float32)
    _, o = lax.scan(step, S0, xs)
    o = jnp.moveaxis(o, 0, 2)
    return jnp.moveaxis(o, 1, 3).reshape(B_, T, H, dv)


def rwkv7_scan(r, decay, k, v, kk, a):
    B_, T, H, N = r.shape

    def step(S, inp):
        r_t, w_t, k_t, v_t, kk_t, a_t = inp
        sa = jnp.einsum('bhvk,bhk->bhv', S, -kk_t)
        S = S * w_t[:, :, None, :] + sa[..., :, None] * (kk_t * a_t)[..., None, :] + v_t[..., :, None] * k_t[..., None, :]
        return S, jnp.einsum('bhvk,bhk->bhv', S, r_t)

    xs = tuple(jnp.moveaxis(t.astype(jnp.float32), 1, 0) for t in (r, decay, k, v, kk, a))
    S0 = jnp.zeros((B_, H, N, N), jnp.float32)
    _, y = lax.scan(step, S0, xs)
    return jnp.moveaxis(y, 0, 1)


def blocked_causal_attention(q, k, v):
    B_, T, H, d = q.shape
    dv = v.shape[-1]
    nb = T // Q_BLOCK
    scale = d ** -0.5
    qb = q.reshape(B_, nb, Q_BLOCK, H, d).transpose(1, 0, 3, 2, 4)
    kf = k.transpose(0, 2, 1, 3)
    vf = v.transpose(0, 2, 1, 3)
    key_pos = jnp.arange(T)

    def one_block(args):
        q_blk, blk = args
        s = jnp.einsum('bhqd,bhkd->bhqk', q_blk, kf).astype(jnp.float32) * scale
        q_pos = blk * Q_BLOCK + jnp.arange(Q_BLOCK)
        s = jnp.where(key_pos[None, :] <= q_pos[:, None], s, -jnp.inf)
        p = jax.nn.softmax(s, axis=-1)
        return jnp.einsum('bhqk,bhkd->bhqd', p, vf)

    o = lax.map(one_block, (qb, jnp.arange(nb)))
    return o.transpose(1, 0, 3, 2, 4).reshape(B_, T, H, dv)


def even_mixer(xn, w_in, conv_a, conv_qkv, a_log, dt_bias, out_norm, w_out):
    B_, T, _ = xn.shape
    a_b, a_c, a_h, qkv, gate_z, b_raw, a_raw = split_cols(xn @ w_in, EV_SPLITS)
    y_a = a_b * causal_depthwise_conv(a_c * a_h, conv_a)
    qkv = jax.nn.silu(causal_depthwise_conv(qkv, conv_qkv))
    q, k, v = [t.reshape(B_, T, GDN_HEADS, GDN_HEAD_DIM) for t in jnp.split(qkv, 3, axis=-1)]
    beta = jax.nn.sigmoid(b_raw.astype(jnp.float32))
    g = -jnp.exp(a_log.astype(jnp.float32)) * jax.nn.softplus(a_raw.astype(jnp.float32) + dt_bias.astype(jnp.float32))
    o = gated_delta_rule_chunked(l2_normalize(q), l2_normalize(k), v.astype(jnp.float32), g, beta)
    o = rms_norm(o, out_norm) * jax.nn.silu(gate_z.reshape(B_, T, GDN_HEADS, GDN_HEAD_DIM).astype(jnp.float32))
    y = jnp.concatenate([y_a, o.reshape(B_, T, GDN_WIDTH).astype(xn.dtype)], axis=-1)
    return y @ w_out


def odd_mixer(xn, cos, sin, v_first, w_in, shift_mu, w0, w2, a0, a2, g2, k_k, k_a, r_k,
              lnx_w, lnx_b, qa_norm, kva_norm, w_uq, w_ukv, q_ln, k_ln, w_out, vres):
    B_, T, _ = xn.shape
    if vres is not None:
        w_in = jnp.concatenate([w_in, vres[0]], axis=1)
    z = xn @ w_in
    z_rwkv = shift_lerp(z[..., :RWKV_SHIFT_COLS], shift_mu).astype(jnp.float32)
    r, k, v, wd, ad, gd = split_cols(z_rwkv, RWKV_SPLITS)
    w_log = -jax.nn.softplus(-(w0 + jnp.tanh(wd) @ w2)) - 0.5
    decay = jnp.exp(-jnp.exp(w_log))
    a = jax.nn.sigmoid(a0 + ad @ a2)
    g = jax.nn.sigmoid(gd) @ g2
    if vres is None:
        v_first = v
    else:
        vd = shift_lerp(z[..., OD_IN_COLS:], vres[1]).astype(jnp.float32)
        v = v + (v_first - v) * jax.nn.sigmoid(vres[2] + vd @ vres[3])
    hshape = (B_, T, RWKV_HEADS, RWKV_HEAD_DIM)
    kk = l2_normalize((k * k_k).reshape(hshape))
    k = k * (1.0 + (a - 1.0) * k_a)
    r_h, k_h, v_h = r.reshape(hshape), k.reshape(hshape), v.reshape(hshape)
    y = rwkv7_scan(r_h, decay.reshape(hshape), k_h, v_h, kk, a.reshape(hshape))
    mu = jnp.mean(y, axis=-1, keepdims=True)
    var = jnp.mean(jnp.square(y - mu), axis=-1, keepdims=True)
    y = ((y - mu) * lax.rsqrt(var + RWKV_LN_EPS)).reshape(B_, T, RWKV_WIDTH) * lnx_w + lnx_b
    y = y + (jnp.sum(r_h * k_h * r_k, axis=-1, keepdims=True) * v_h).reshape(B_, T, RWKV_WIDTH)
    y_rwkv = y * g
    cq, ckv, k_rope = split_cols(z[..., RWKV_SHIFT_COLS:OD_IN_COLS], MLA_SPLITS)
    q = (rms_norm(cq, qa_norm) @ w_uq).reshape(B_, T, MLA_HEADS, MLA_QK_DIM)
    kv = (rms_norm(ckv, kva_norm) @ w_ukv).reshape(B_, T, MLA_HEADS, MLA_NOPE + MLA_V)
    k_nope, v_mla = jnp.split(kv, [MLA_NOPE], axis=-1)
    k_rope_h = jnp.broadcast_to(k_rope[:, :, None, :], (B_, T, MLA_HEADS, MLA_ROPE))
    k_mla = rms_norm(jnp.concatenate([k_nope, k_rope_h], axis=-1), k_ln)
    q = rms_norm(q, q_ln)
    q = jnp.concatenate([q[..., :MLA_NOPE].astype(jnp.float32), apply_rope(q[..., MLA_NOPE:], cos, sin)], axis=-1)
    k_mla = jnp.concatenate([k_mla[..., :MLA_NOPE].astype(jnp.float32), apply_rope(k_mla[..., MLA_NOPE:], cos, sin)], axis=-1)
    o = blocked_causal_attention(q, k_mla, v_mla.astype(jnp.float32))
    y_mix = jnp.concatenate([y_rwkv.astype(xn.dtype), o.reshape(B_, T, MLA_WIDTH).astype(xn.dtype)], axis=-1)
    return y_mix @ w_out, v_first


def setup_inputs(seed: int = 0) -> dict:
    key = jax.random.key(seed)
    keys = jax.random.split(key, 48)
    counter = [0]

    def nk():
        counter[0] += 1
        return keys[counter[0] - 1]

    def nrm(shape, scale):
        return jax.random.normal(nk(), shape, jnp.float32) * scale

    def gain(shape):
        return 1.0 + nrm(shape, 0.02)

    def unif(shape, lo, hi):
        return jax.random.uniform(nk(), shape, jnp.float32, lo, hi)

    NE, NO, NV = N_EVEN, N_ODD, N_ODD - 1
    x = jax.random.normal(nk(), (BATCH, SEQ, D_MODEL), jnp.float32)
    positions = jax.random.randint(nk(), (BATCH, 1), 0, 1024, dtype=jnp.int32) + jnp.arange(SEQ, dtype=jnp.int32)[None, :]
    dt = jnp.exp(unif((NE, GDN_HEADS), math.log(1e-3), math.log(1e-1)))
    return {
        'x': x,
        'positions': positions,
        'norm_mix': gain((DEPTH, D_MODEL)),
        'norm_ffn': gain((DEPTH, D_MODEL)),
        'ffn_w_gu': nrm((DEPTH, D_MODEL, 2 * D_FF), D_MODEL ** -0.5),
        'ffn_w_down': nrm((DEPTH, D_FF, D_MODEL), D_FF ** -0.5),
        'ev_w_in': nrm((NE, D_MODEL, EV_IN_COLS), D_MODEL ** -0.5),
        'ev_conv_a': nrm((NE, A_CONV, A_WIDTH), A_CONV ** -0.5),
        'ev_conv_qkv': nrm((NE, GDN_CONV, 3 * GDN_WIDTH), GDN_CONV ** -0.5),
        'ev_a_log': jnp.log(unif((NE, GDN_HEADS), 1.0, 16.0)),
        'ev_dt_bias': dt + jnp.log(-jnp.expm1(-dt)),
        'ev_out_norm': gain((NE, GDN_HEAD_DIM)),
        'ev_w_out': nrm((NE, EV_MIX_WIDTH, D_MODEL), EV_MIX_WIDTH ** -0.5),
        'od_w_in': nrm((NO, D_MODEL, OD_IN_COLS), D_MODEL ** -0.5),
        'od_shift_mu': unif((NO, RWKV_SHIFT_COLS), 0.0, 1.0),
        'od_w0': unif((NO, RWKV_WIDTH), -6.0, -1.0),
        'od_w2': nrm((NO, RWKV_W_LORA, RWKV_WIDTH), 0.1),
        'od_a0': nrm((NO, RWKV_WIDTH), 0.1),
        'od_a2': nrm((NO, RWKV_A_LORA, RWKV_WIDTH), 0.5 * RWKV_A_LORA ** -0.5),
        'od_g2': nrm((NO, RWKV_G_LORA, RWKV_WIDTH), RWKV_G_LORA ** -0.5),
        'od_k_k': 0.85 + nrm((NO, RWKV_WIDTH), 0.05),
        'od_k_a': 1.0 + nrm((NO, RWKV_WIDTH), 0.05),
        'od_r_k': nrm((NO, RWKV_HEADS, RWKV_HEAD_DIM), 0.1),
        'od_lnx_w': gain((NO, RWKV_WIDTH)),
        'od_lnx_b': nrm((NO, RWKV_WIDTH), 0.01),
        'od_vres_w1': nrm((NV, D_MODEL, RWKV_V_LORA), D_MODEL ** -0.5),
        'od_vres_mu': unif((NV, RWKV_V_LORA), 0.0, 1.0),
        'od_vres_v0': nrm((NV, RWKV_WIDTH), 0.1),
        'od_vres_v2': nrm((NV, RWKV_V_LORA, RWKV_WIDTH), 0.5 * RWKV_V_LORA ** -0.5),
        'od_qa_norm': gain((NO, MLA_Q_LORA)),
        'od_kva_norm': gain((NO, MLA_KV_LORA)),
        'od_w_uq': nrm((NO, MLA_Q_LORA, MLA_HEADS * MLA_QK_DIM), MLA_Q_LORA ** -0.5),
        'od_w_ukv': nrm((NO, MLA_KV_LORA, MLA_HEADS * (MLA_NOPE + MLA_V)), MLA_KV_LORA ** -0.5),
        'od_q_ln': gain((NO, MLA_QK_DIM)),
        'od_k_ln': gain((NO, MLA_QK_DIM)),
        'od_w_out': nrm((NO, OD_MIX_WIDTH, D_MODEL), OD_MIX_WIDTH ** -0.5),
    }


def reference(x, positions, norm_mix, norm_ffn, ffn_w_gu, ffn_w_down,
              ev_w_in, ev_conv_a, ev_conv_qkv, ev_a_log, ev_dt_bias, ev_out_norm, ev_w_out,
              od_w_in, od_shift_mu, od_w0, od_w2, od_a0, od_a2, od_g2, od_k_k, od_k_a, od_r_k,
              od_lnx_w, od_lnx_b, od_vres_w1, od_vres_mu, od_vres_v0, od_vres_v2,
              od_qa_norm, od_kva_norm, od_w_uq, od_w_ukv, od_q_ln, od_k_ln, od_w_out):
    cos, sin = rope_tables(positions)
    v_first = None
    for layer in range(DEPTH):
        xn = rms_norm(x, norm_mix[layer])
        if layer % 2 == 0:
            e = layer // 2
            h = even_mixer(xn, ev_w_in[e], ev_conv_a[e], ev_conv_qkv[e], ev_a_log[e],
                           ev_dt_bias[e], ev_out_norm[e], ev_w_out[e])
        else:
            o = layer // 2
            vres = None if o == 0 else (od_vres_w1[o - 1], od_vres_mu[o - 1], od_vres_v0[o - 1], od_vres_v2[o - 1])
            h, v_first = odd_mixer(xn, cos, sin, v_first, od_w_in[o], od_shift_mu[o], od_w0[o], od_w2[o],
                                   od_a0[o], od_a2[o], od_g2[o], od_k_k[o], od_k_a[o], od_r_k[o],
                                   od_lnx_w[o], od_lnx_b[o], od_qa_norm[o], od_kva_norm[o], od_w_uq[o],
                                   od_w_ukv[o], od_q_ln[o], od_k_ln[o], od_w_out[o], vres)
        x = x + h
        x = x + swiglu(rms_norm(x, norm_ffn[layer]), ffn_w_gu[layer], ffn_w_down[layer])
    return x
```

```python
import numpy as np
import concourse.bass as bass
import concourse.mybir as mybir
from concourse.bass_utils import run_bass_kernel_spmd
from contextlib import ExitStack

F32 = mybir.dt.float32
BF16 = mybir.dt.bfloat16
I32 = mybir.dt.int32
AF = mybir.ActivationFunctionType
ALU = mybir.AluOpType
AX = mybir.AxisListType

D = 1024
DFF = 2816
NL = 4


class Buf:
    __slots__ = ("name", "w", "r", "ap", "excl")

    def __init__(self, name, ap=None, excl=False):
        self.name = name
        self.w = []
        self.r = {}
        self.ap = ap
        self.excl = excl

    def __getitem__(self, k):
        return V(self.ap[k], [self])

    def v(self, ap):
        return V(ap, [self])

    @property
    def bs(self):
        return [self]


class V:
    __slots__ = ("ap", "bs")

    def __init__(self, ap, bs=()):
        self.ap = ap
        self.bs = list(bs)


class Tok:
    __slots__ = ("sem", "val", "know", "q", "dma")

    def __init__(self, sem, val, know, q, dma):
        self.sem = sem
        self.val = val
        self.know = know
        self.q = q
        self.dma = dma


class Q:
    def __init__(self, name, sem, ring):
        self.name = name
        self.sem = sem
        self.cnt = 0
        self.ring = ring
        self.ring_cnt = [0] * len(ring)
        self.ring_tok = [None] * len(ring)
        self.ndma = 0
        self.seen = {}
        self.ops = []
        self.last_tok = None


class Prog:
    SEM_ROLL = 30000

    def __init__(self, nc, es, ring=8):
        self.nc = nc
        self.es = es
        self.nsem = 0
        self.q = {}
        for n in ["pe", "act", "dve", "pool", "sp"]:
            sem = es.enter_context(nc.semaphore("s_" + n))
            r = []
            if n in ("sp", "pool", "act"):
                r = [es.enter_context(nc.semaphore("r_%s%d" % (n, i))) for i in range(ring)]
            self.q[n] = Q(n, sem, r)
        self.nops = 0

    def _need(self, q, tok, waits):
        k = id(tok.sem)
        if q.seen.get(k, 0) >= tok.val:
            return
        if k not in waits or waits[k][0] < tok.val:
            waits[k] = (tok.val, tok.sem)
        new = dict(q.seen)
        for kk, v in tok.know.items():
            if new.get(kk, 0) < v:
                new[kk] = v
        new[k] = tok.val
        q.seen = new

    def op(self, qn, fn, reads=(), writes=(), accs=(), dma=False):
        q = self.q[qn]
        waits = {}
        for b in reads:
            for t in b.w:
                if t.q is q and not t.dma and qn == "pe":
                    continue
                self._need(q, t, waits)
            if b.excl:
                for t in b.r.values():
                    if t.q is not q:
                        self._need(q, t, waits)
        for b in list(writes) + list(accs):
            for t in b.r.values():
                if t.q is q and not t.dma:
                    continue
                self._need(q, t, waits)
        for b in writes:
            for t in b.w:
                if t.q is q and not t.dma:
                    continue
                self._need(q, t, waits)
        for b in accs:
            for t in b.w:
                if t.dma or (t.q is not q):
                    self._need(q, t, waits)
        if dma:
            slot = q.ndma % len(q.ring)
            q.ndma += 1
            if q.ring_tok[slot] is not None:
                self._need(q, q.ring_tok[slot], waits)
            q.ring_cnt[slot] += 16
            tok = Tok(q.ring[slot], q.ring_cnt[slot], q.seen, q, True)
            q.ring_tok[slot] = tok
            inc = (q.ring[slot], 16)
        else:
            if q.cnt >= self.SEM_ROLL:
                self.nsem += 1
                q.sem = self.es.enter_context(self.nc.semaphore("s_%s_%d" % (qn, self.nsem)))
                q.cnt = 0
            q.cnt += 1
            tok = Tok(q.sem, q.cnt, q.seen, q, False)
            q.last_tok = tok
            inc = (q.sem, 1)
        q.ops.append(([(s, v) for (v, s) in waits.values()], fn, inc))
        self.nops += 1
        key = qn + ("d" if dma else "")
        for b in reads:
            b.r[key] = tok
        for b in writes:
            b.w = [tok]
            b.r = {}
        for b in accs:
            if not dma:
                b.w = [t for t in b.w if t.dma or t.q is not q]
            b.w.append(tok)
        return tok

    def barrier(self):
        toks = []
        for q in self.q.values():
            if q.last_tok is not None:
                toks.append(q.last_tok)
            for t in q.ring_tok:
                if t is not None:
                    toks.append(t)
        for qn, q in self.q.items():
            waits = {}
            for t in toks:
                if t.q is q and not t.dma and qn == "pe":
                    continue
                self._need(q, t, waits)
            wl = [(s, v) for (v, s) in waits.values()]
            if wl:
                q.ops.append((wl, None, None))

    def emit(self):
        nc = self.nc
        qs = self.q
        with nc.Block() as block:
            def run(eng, q):
                for wl, fn, inc in q.ops:
                    for s, v in wl:
                        eng.wait_ge(s, v)
                    if fn is not None:
                        fn(eng).then_inc(inc[0], inc[1])
                q.ops = []

            @block.tensor
            def _(e):
                run(e, qs["pe"])

            @block.scalar
            def _(e):
                run(e, qs["act"])

            @block.vector
            def _(e):
                run(e, qs["dve"])

            @block.gpsimd
            def _(e):
                run(e, qs["pool"])

            @block.sync
            def _(e):
                run(e, qs["sp"])


class RR:
    def __init__(self, bufs):
        self.bufs = bufs
        self.i = 0

    def next(self):
        b = self.bufs[self.i % len(self.bufs)]
        self.i += 1
        return b


def make_consts():
    c = {}
    i = np.arange(128)
    c["ident"] = np.eye(128, dtype=np.float32)
    c["utri"] = (i[:, None] <= i[None, :]).astype(np.float32)
    c["ustrict"] = (i[:, None] < i[None, :]).astype(np.float32)
    c["ones"] = np.ones((128, 128), np.float32)
    bo = np.zeros((128, 128), np.float32)
    bo[:64, :64] = 1
    bo[64:, 64:] = 1
    c["blkones"] = bo
    return np.stack([c[k] for k in ["ident", "utri", "ustrict", "ones", "blkones"]], 0)


def make_ropec():
    return (10000.0 ** (-np.arange(0, 32, 2, dtype=np.float32) / 32)).astype(np.float32).reshape(1, 16)


def make_masks():
    i = np.arange(128)
    t = i[:, None]
    s = i[None, :]
    out = np.zeros((7, 2, 128, 128), np.int32)
    for l in range(7):
        b = 1 << l
        m = ((t // (2 * b)) == (s // (2 * b))) & ((t % (2 * b)) >= b) & ((s % (2 * b)) < b)
        out[l, 0] = m
        out[l, 1] = m.T
    return out


def build(T, plan, dbg=(), dbg_in=()):
    nc = bass.Bass("TRN2", target_bir_lowering=False)
    TT = T // 128

    def ext(name, shape, dt=F32):
        return nc.dram_tensor(name, list(shape), dt, kind="ExternalInput").ap()

    def scr(name, shape, dt):
        kind = "ExternalOutput" if name in dbg else ("ExternalInput" if name in dbg_in else "Internal")
        return nc.dram_tensor(name, list(shape), dt, kind=kind).ap()

    wshapes = dict(
        norm_mix=[4, D], norm_ffn=[4, D], ffn_w_gu=[4, D, 2 * DFF], ffn_w_down=[4, DFF, D],
        ev_w_in=[2, D, 3852], ev_conv_a=[2, 3, 256], ev_conv_qkv=[2, 4, 2304], ev_a_log=[2, 6], ev_dt_bias=[2, 6],
        ev_out_norm=[2, 128], ev_w_out=[2, D, D], od_w_in=[2, D, 2592], od_shift_mu=[2, 1792], od_w0=[2, 512],
        od_w2=[2, 64, 512], od_a0=[2, 512], od_a2=[2, 64, 512], od_g2=[2, 128, 512], od_k_k=[2, 512], od_k_a=[2, 512],
        od_r_k=[2, 8, 64], od_lnx_w=[2, 512], od_lnx_b=[2, 512], od_vres_w1=[1, D, 32], od_vres_mu=[1, 32],
        od_vres_v0=[1, 512], od_vres_v2=[1, 32, 512], od_qa_norm=[2, 512], od_kva_norm=[2, 256], od_w_uq=[2, 512, 768],
        od_w_ukv=[2, 256, 1024], od_q_ln=[2, 96], od_k_ln=[2, 96], od_w_out=[2, D, D],
        x=[T, D], consts=[5, 128, 128])

    class Lazy(dict):
        def __missing__(self, k):
            if k == "positions":
                v = ext(k, [T, 1], I32)
            elif k == "masks":
                v = ext(k, [7, 2, 128, 128], I32)
            elif k == "ropec":
                v = ext(k, [1, 16])
            else:
                v = ext(k, wshapes[k])
            self[k] = v
            return v

    I = Lazy()

    class LazyW(dict):
        def __missing__(self, k):
            v = scr("b_" + k, wshapes[k], BF16)
            self[k] = v
            return v
    Wb = LazyW()
    wneed = set()
    for st in plan:
        if st[0] == "outproj":
            wneed.add(st[1])
        if st[0] == "ffn":
            wneed |= {"ffn_w_gu", "ffn_w_down"}
        if st[0] in ("evin",):
            wneed |= {"ev_w_in"}
        if st[0] in ("odin",):
            wneed |= {"od_w_in", "od_w_uq", "od_w_ukv", "od_vres_w1", "od_w2", "od_a2", "od_g2", "od_vres_v2"}
    wneed = sorted(wneed)
    out_ap = nc.dram_tensor("out", [T, D], F32, kind="ExternalOutput").ap()

    xs = scr("xs", [T, D], F32)
    yT = scr("yT", [D, T], BF16)
    with ExitStack() as es:
        P = Prog(nc, es)

        def mk_sb(stack):
            cnt = [0]

            def sb(shape, dt, name=None):
                cnt[0] += 1
                nm = "%s_%d" % (name or "t", nc.next_id())
                t = stack.enter_context(nc.sbuf_tensor(nm, list(shape), dt))
                return Buf(nm, t.ap())
            return sb

        sbg = mk_sb(es)
        banks = []
        for i in range(8):
            t = es.enter_context(nc.psum_tensor("bank%d" % i, [128, 512], F32))
            banks.append(Buf("bank%d" % i, t.ap(), excl=True))

        def bl(*vs):
            o = []
            for v in vs:
                if isinstance(v, V):
                    o += v.bs
            return o

        def mm(out, lhsT, rhs, start=True, stop=True):
            P.op("pe", lambda e: e.matmul(out.ap, lhsT.ap, rhs.ap, start=start, stop=stop),
                 reads=bl(lhsT, rhs), writes=out.bs if start else [], accs=[] if start else out.bs)

        def tr(out, in_, ident):
            P.op("pe", lambda e: e.transpose(out.ap, in_.ap, ident.ap), reads=bl(in_, ident), writes=out.bs)

        def _w(out, acc):
            return dict(writes=[] if acc else out.bs, accs=out.bs if acc else [])

        def act(out, in_, func, bias=None, scale=None, accum=None, acc=False):
            kw = {}
            if bias is not None:
                kw["bias"] = bias.ap if isinstance(bias, V) else bias
            if scale is not None:
                kw["scale"] = scale.ap if isinstance(scale, V) else scale
            if accum is not None:
                kw["accum_out"] = accum.ap
            wr = _w(out, acc)
            if accum is not None:
                wr["accs"] = list(wr["accs"]) + accum.bs
            P.op("act", lambda e: e.activation(out.ap, in_.ap, func, **kw), reads=bl(in_, bias, scale), **wr)

        def tt(q, out, a, b, op, acc=False):
            P.op(q, lambda e: e.tensor_tensor(out.ap, a.ap, b.ap, op), reads=bl(a, b), **_w(out, acc))

        def ts(q, out, a, s1, s2, op0, op1=None, acc=False):
            s1a = s1.ap if isinstance(s1, V) else s1
            s2a = s2.ap if isinstance(s2, V) else s2
            if op1 is None:
                P.op(q, lambda e: e.tensor_scalar(out.ap, a.ap, s1a, None, op0), reads=bl(a, s1), **_w(out, acc))
            else:
                P.op(q, lambda e: e.tensor_scalar(out.ap, a.ap, s1a, s2a, op0, op1), reads=bl(a, s1, s2), **_w(out, acc))

        def stt(q, out, a, s, b, op0, op1, acc=False):
            sa = s.ap if isinstance(s, V) else s
            P.op(q, lambda e: e.scalar_tensor_tensor(out.ap, a.ap, sa, b.ap, op0, op1), reads=bl(a, s, b), **_w(out, acc))

        def cp(q, out, in_, acc=False):
            if q == "act":
                P.op("act", lambda e: e.copy(out.ap, in_.ap), reads=bl(in_), **_w(out, acc))
            else:
                P.op(q, lambda e: e.tensor_copy(out.ap, in_.ap), reads=bl(in_), **_w(out, acc))

        def red(q, out, in_, op, acc=False):
            P.op(q, lambda e: e.tensor_reduce(out.ap, in_.ap, AX.X, op), reads=bl(in_), **_w(out, acc))

        def recip(out, in_, acc=False):
            P.op("dve", lambda e: e.reciprocal(out.ap, in_.ap), reads=bl(in_), **_w(out, acc))

        def cpred(out, mask, data):
            P.op("dve", lambda e: e.copy_predicated(out.ap, mask.ap, data.ap), reads=bl(mask, data), accs=out.bs)

        def dma(out, in_, q="sp", acc=False, slow=False):
            if slow:
                P.op(q, lambda e: e.dma_start(out=out.ap, in_=in_.ap, allow_slow_non_contiguous=True), reads=bl(in_), dma=True,
                     **_w(out, acc))
            else:
                P.op(q, lambda e: e.dma_start(out=out.ap, in_=in_.ap), reads=bl(in_), dma=True, **_w(out, acc))

        def DV(ap):
            return V(ap, [])

        def phase_end():
            P.barrier()
            P.emit()

        rr_evac = [0]

        def evac(out, in_, acc=False):
            rr_evac[0] += 1
            cp("act" if rr_evac[0] % 2 else "dve", out, in_, acc=acc)

        cst = sbg([128, 5, 128], F32, "cst")
        dma(cst[:], DV(I["consts"].rearrange("c p n -> p c n")))
        ident_f = cst[:, 0, :]
        utri_f = cst[:, 1, :]
        ustr_f = cst[:, 2, :]
        ones_f = cst[:, 3, :]
        cstb = sbg([128, 5, 128], BF16, "cstb")
        cp("dve", cstb[:], cst[:])
        ident_b = cstb[:, 0, :]
        ones_b = cstb[:, 3, :]
        blk_b = cstb[:, 4, :]
        msk = sbg([128, 14, 128], I32, "msk")
        dma(msk[:], DV(I["masks"].rearrange("l j p n -> p (l j) n")))
        epsb = sbg([128, 4], F32, "epsb")
        P.op("pool", lambda e: e.memset(epsb.ap[:, 0:1], 1e-6), writes=[epsb])
        P.op("pool", lambda e: e.memset(epsb.ap[:, 1:2], 64e-5), accs=[epsb])
        P.op("pool", lambda e: e.memset(epsb.ap[:, 2:3], 0.0), accs=[epsb])
        P.op("pool", lambda e: e.memset(epsb.ap[:, 3:4], 1.0), accs=[epsb])
        eps6 = epsb[:, 0:1]

        with ExitStack() as ph:
            sb = mk_sb(ph)
            CW = 2048
            fbuf = RR([sb([128, CW], F32, "cf") for _ in range(3)])
            bbuf = RR([sb([128, CW], BF16, "cb") for _ in range(3)])
            n = 0
            for k in wneed:
                dst = Wb[k]
                src = I[k]
                R = 1
                for s_ in wshapes[k][:-1]:
                    R *= s_
                C = wshapes[k][-1]
                s2 = src.flatten_outer_dims() if len(wshapes[k]) > 2 else src
                d2 = dst.flatten_outer_dims() if len(wshapes[k]) > 2 else dst
                for r0 in range(0, R, 128):
                    rn = min(128, R - r0)
                    for c0 in range(0, C, CW):
                        cn = min(CW, C - c0)
                        f = fbuf.next()
                        b = bbuf.next()
                        dma(f[0:rn, 0:cn], DV(s2[r0:r0 + rn, c0:c0 + cn]))
                        cp(["dve", "act", "pool"][n % 3], b[0:rn, 0:cn], f[0:rn, 0:cn])
                        dma(DV(d2[r0:r0 + rn, c0:c0 + cn]), b[0:rn, 0:cn], q="pool")
                        n += 1
            for t0 in range(0, T, 512):
                dma(DV(xs[t0:t0 + 512, :]), DV(I["x"][t0:t0 + 512, :]))
            phase_end()

        def rmsnorm_tile(sb_tmp, xin, gain_row, out_bf, n):
            junk, ss = sb_tmp
            act(junk[:, 0:n], xin, AF.Square, accum=ss[:, 0:1])
            act(ss[:, 1:2], ss[:, 0:1], AF.Sqrt, bias=eps6, scale=1.0 / n, acc=True)
            recip(ss[:, 1:2], ss[:, 1:2], acc=True)
            stt("dve", out_bf, xin, ss[:, 1:2], gain_row, ALU.mult, ALU.mult)

        def phase_outproj(wname, li):
            with ExitStack() as ph:
                sb = mk_sb(ph)
                wo = sb([128, 8, D], BF16, "wo")
                dma(wo[:], DV(Wb[wname][li].rearrange("(k p) n -> p k n", p=128)))
                ybl = RR([sb([128, 8, 512], BF16, "yb") for _ in range(2)])
                xbl = RR([sb([128, D], F32, "xb") for _ in range(3)])
                pbs = RR(banks[0:4])
                for t0 in range(0, T, 512):
                    yb = ybl.next()
                    dma(yb[:], DV(yT.rearrange("(k p) t -> p k t", p=128)[:, :, t0:t0 + 512]))
                    for j in range(4):
                        xb = xbl.next()
                        r0 = t0 + j * 128
                        dma(xb[:], DV(xs[r0:r0 + 128, :]))
                        for h in range(2):
                            pb = pbs.next()
                            for k in range(8):
                                mm(pb[:, :], yb[:, k, j * 128:(j + 1) * 128], wo[:, k, h * 512:(h + 1) * 512],
                                   start=(k == 0), stop=(k == 7))
                            tt("dve", xb[:, h * 512:(h + 1) * 512], xb[:, h * 512:(h + 1) * 512], pb[:, :], ALU.add, acc=True)
                        dma(DV(xs[r0:r0 + 128, :]), xb[:], q="pool")
                phase_end()

        def phase_ffn(L, final):
            with ExitStack() as ph:
                sb = mk_sb(ph)
                wgu = sb([128, 8, 2 * DFF], BF16, "wgu")
                for c in range(4):
                    dma(wgu[:, :, c * 1408:(c + 1) * 1408],
                        DV(Wb["ffn_w_gu"][L].rearrange("(k p) n -> p k n", p=128)[:, :, c * 1408:(c + 1) * 1408]), acc=(c > 0))
                wd = sb([128, 22, D], BF16, "wd")
                dma(wd[:], DV(Wb["ffn_w_down"][L].rearrange("(k p) n -> p k n", p=128)))
                gain = sb([128, D], F32, "gain")
                dma(gain[:], DV(I["norm_ffn"][L:L + 1, :].partition_broadcast(128)))
                xbl = RR([sb([128, D], F32, "xb") for _ in range(4)])
                xnl = RR([sb([128, D], BF16, "xn") for _ in range(2)])
                xnTl = RR([sb([128, 8, 256], BF16, "xnT") for _ in range(2)])
                hTl = RR([sb([128, 22, 256], BF16, "hT") for _ in range(1)])
                sgl = RR([sb([128, 256], F32, "sg") for _ in range(2)])
                junk = sb([128, D], F32, "junk")
                ssl = RR([sb([128, 2], F32, "ss") for _ in range(4)])
                ptr = RR(banks[0:2])
                pgu = RR(banks[2:6])
                pdn = RR(banks[6:8])
                for t0 in range(0, T, 256):
                    xnT = xnTl.next()
                    xbs = []
                    for j in range(2):
                        xb = xbl.next()
                        xbs.append(xb)
                        r0 = t0 + j * 128
                        dma(xb[:], DV(xs[r0:r0 + 128, :]))
                        xn = xnl.next()
                        rmsnorm_tile((junk, ssl.next()), xb[:], gain[:], xn[:], D)
                        for kq in range(2):
                            pt = ptr.next()
                            ptb = pt.v(pt.ap.bitcast(BF16))
                            for k4 in range(4):
                                k = kq * 4 + k4
                                tr(pt.v(pt.ap.bitcast(BF16)[:, k4 * 128:(k4 + 1) * 128]), xn[:, k * 128:(k + 1) * 128], ident_b)
                            evac(xnT.v(xnT.ap[:, kq * 4:(kq + 1) * 4, j * 128:(j + 1) * 128]),
                                 pt.v(pt.ap.bitcast(BF16)[:, 0:512].rearrange("p (a b) -> p a b", a=4)),
                                 acc=not (j == 0 and kq == 0))
                    hT = hTl.next()
                    for c in range(22):
                        pg = pgu.next()
                        pu = pgu.next()
                        for k in range(8):
                            mm(pg[:, 0:256], wgu[:, k, c * 128:(c + 1) * 128], xnT[:, k, :], start=(k == 0), stop=(k == 7))
                        for k in range(8):
                            mm(pu[:, 0:256], wgu[:, k, DFF + c * 128:DFF + (c + 1) * 128], xnT[:, k, :], start=(k == 0), stop=(k == 7))
                        sg = sgl.next()
                        act(sg[:], pg[:, 0:256], AF.Silu)
                        tt("dve", hT[:, c, :], sg[:], pu[:, 0:256], ALU.mult, acc=(c > 0))
                    for j in range(2):
                        xb = xbs[j]
                        r0 = t0 + j * 128
                        for h in range(2):
                            pd = pdn.next()
                            for c in range(22):
                                mm(pd[:, :], hT[:, c, j * 128:(j + 1) * 128], wd[:, c, h * 512:(h + 1) * 512],
                                   start=(c == 0), stop=(c == 21))
                            tt("dve", xb[:, h * 512:(h + 1) * 512], xb[:, h * 512:(h + 1) * 512], pd[:, :], ALU.add, acc=True)
                        dst = out_ap if final else xs
                        dma(DV(dst[r0:r0 + 128, :]), xb[:], q="pool")
                phase_end()


        def norm_transpose(xb, gain, xn, xnT, col0, tmp, ptr, first):
            rmsnorm_tile(tmp, xb[:], gain[:], xn[:], D)
            for kq in range(2):
                pt = ptr.next()
                for k4 in range(4):
                    k = kq * 4 + k4
                    tr(pt.v(pt.ap.bitcast(BF16)[:, k4 * 128:(k4 + 1) * 128]), xn[:, k * 128:(k + 1) * 128], ident_b)
                evac(xnT.v(xnT.ap[:, kq * 4:(kq + 1) * 4, col0:col0 + 128]),
                     pt.v(pt.ap.bitcast(BF16)[:, 0:512].rearrange("p (a b) -> p a b", a=4)),
                     acc=not (first and kq == 0))

        qkvT = scr("qkvT", [18, 128, T], BF16)
        gz = scr("gz", [T, 768], F32)
        bgd = scr("bgd", [T, 12], F32)

        def phase_evin(e, L):
            with ExitStack() as ph:
                sb = mk_sb(ph)
                NC_ = 3852
                win = sb([128, 8, NC_], BF16, "win")
                for c in range(4):
                    dma(win[:, :, c * 963:(c + 1) * 963],
                        DV(Wb["ev_w_in"][e].rearrange("(k p) n -> p k n", p=128)[:, :, c * 963:(c + 1) * 963]), acc=(c > 0))
                gain = sb([128, D], F32, "gain")
                dma(gain[:], DV(I["norm_mix"][L:L + 1, :].partition_broadcast(128)))
                cwa = sb([128, 2, 3], F32, "cwa")
                for j_ in range(3):
                    dma(cwa[:, :, j_], DV(I["ev_conv_a"][e, j_].rearrange("(c p) -> p c", p=128)), slow=True, acc=(j_ > 0))
                cwq = sb([128, 18, 4], F32, "cwq")
                for j_ in range(4):
                    dma(cwq[:, :, j_], DV(I["ev_conv_qkv"][e, j_].rearrange("(c p) -> p c", p=128)), slow=True, acc=(j_ > 0))
                prm = sb([128, 12], F32, "prm")
                dma(prm[:, 0:6], DV(I["ev_a_log"][e:e + 1, :].partition_broadcast(128)))
                dma(prm[:, 6:12], DV(I["ev_dt_bias"][e:e + 1, :].partition_broadcast(128)), acc=True)
                act(prm[:, 0:6], prm[:, 0:6], AF.Exp, acc=True)
                ts("dve", prm[:, 0:6], prm[:, 0:6], -1.0, None, ALU.mult, acc=True)
                zs = sb([128, 24, 259], F32, "zs")
                P.op("pool", lambda e_: e_.memset(zs.ap[:, :, 0:3], 0.0), writes=[zs])
                pr = sb([128, 2, 259], F32, "pr")
                P.op("pool", lambda e_: e_.memset(pr.ap[:, :, 0:3], 0.0), writes=[pr])
                xbl = RR([sb([128, D], F32, "xb") for _ in range(3)])
                xnl = RR([sb([128, D], BF16, "xn") for _ in range(2)])
                xnTl = RR([sb([128, 8, 256], BF16, "xnT") for _ in range(2)])
                junk = sb([128, D], F32, "junk")
                ssl = RR([sb([128, 2], F32, "ss") for _ in range(4)])
                cvl = RR([sb([128, 256], F32, "cv") for _ in range(3)])
                cv2l = RR([sb([128, 256], F32, "cv2") for _ in range(3)])
                sql = RR([sb([128, 256], F32, "sq") for _ in range(2)])
                rnl = RR([sb([128, 256], F32, "rn") for _ in range(2)])
                obl = RR([sb([128, 256], BF16, "ob") for _ in range(4)])
                yal = RR([sb([128, 2, 256], BF16, "ya") for _ in range(2)])
                zgl = RR([sb([128, 768], F32, "zg") for _ in range(2)])
                bgl = RR([sb([128, 12], F32, "bg") for _ in range(2)])
                ptr = RR(banks[0:2])
                pfm = RR(banks[2:5])
                pss = RR(banks[5:6])
                ptk = RR(banks[6:8])
                for t0 in range(0, T, 256):
                    xnT = xnTl.next()
                    for j in range(2):
                        xb = xbl.next()
                        dma(xb[:], DV(xs[t0 + j * 128:t0 + (j + 1) * 128, :]))
                        norm_transpose(xb, gain, xnl.next(), xnT, j * 128, (junk, ssl.next()), ptr, j == 0)
                    for c in range(24):
                        pf = pfm.next()
                        for k in range(8):
                            mm(pf[:, 0:256], win[:, k, c * 128:(c + 1) * 128], xnT[:, k, :], start=(k == 0), stop=(k == 7))
                        evac(zs[:, c, 3:259], pf[:, 0:256], acc=True)
                    tt("pool", pr[:, :, 3:259], zs[:, 2:4, 3:259], zs[:, 4:6, 3:259], ALU.mult, acc=True)
                    ya = yal.next()
                    for c in range(2):
                        cv = cvl.next()
                        ts("dve", cv[:], pr[:, c, 1:257], cwa[:, c, 0:1], None, ALU.mult)
                        stt("dve", cv[:], pr[:, c, 2:258], cwa[:, c, 1:2], cv[:], ALU.mult, ALU.add)
                        stt("dve", cv[:], pr[:, c, 3:259], cwa[:, c, 2:3], cv[:], ALU.mult, ALU.add)
                        tt("dve", ya[:, c, :], cv[:], zs[:, c, 3:259], ALU.mult, acc=(c > 0))
                    dma(DV(yT[0:256, t0:t0 + 256].rearrange("(c p) t -> p c t", p=128)), ya[:], q="pool")
                    cp("pool", pr[:, :, 0:3], pr[:, :, 256:259], acc=True)
                    for c in range(18):
                        zc = 6 + c
                        cv = cvl.next()
                        eng = "dve"
                        ts(eng, cv[:], zs[:, zc, 0:256], cwq[:, c, 0:1], None, ALU.mult)
                        for j in range(1, 4):
                            stt(eng, cv[:], zs[:, zc, j:j + 256], cwq[:, c, j:j + 1], cv[:], ALU.mult, ALU.add)
                        cv2 = cv2l.next()
                        act(cv2[:], cv[:], AF.Silu)
                        ob = obl.next()
                        if c < 12:
                            sq = sql.next()
                            tt("pool", sq[:], cv2[:], cv2[:], ALU.mult)
                            pq = pss.next()
                            mm(pq[:, 0:256], ones_f, sq[:])
                            rn = rnl.next()
                            act(rn[:], pq[:, 0:256], AF.Sqrt, bias=eps6, scale=1.0)
                            recip(rn[:], rn[:])
                            if c < 6:
                                stt("dve", ob[:], cv2[:], 128.0 ** -0.5, rn[:], ALU.mult, ALU.mult)
                            else:
                                tt("dve", ob[:], cv2[:], rn[:], ALU.mult)
                        else:
                            cp("pool", ob[:], cv2[:])
                        dma(DV(qkvT[c, :, t0:t0 + 256]), ob[:], q="pool")
                    cp("pool", zs[:, :, 0:3], zs[:, :, 256:259], acc=True)
                    for j in range(2):
                        p1 = ptk.next()
                        p2 = ptk.next()
                        for k in range(8):
                            mm(p1[:, :], xnT[:, k, j * 128:(j + 1) * 128], win[:, k, 3072:3584], start=(k == 0), stop=(k == 7))
                        for k in range(8):
                            mm(p2[:, 0:268], xnT[:, k, j * 128:(j + 1) * 128], win[:, k, 3584:3852], start=(k == 0), stop=(k == 7))
                        zg = zgl.next()
                        act(zg[:, 0:512], p1[:, :], AF.Silu)
                        act(zg[:, 512:768], p2[:, 0:256], AF.Silu, acc=True)
                        bg = bgl.next()
                        act(bg[:, 0:6], p2[:, 256:262], AF.Sigmoid)
                        tt("dve", bg[:, 6:12], p2[:, 262:268], prm[:, 6:12], ALU.add, acc=True)
                        act(bg[:, 6:12], bg[:, 6:12], AF.Exp, acc=True)
                        act(bg[:, 6:12], bg[:, 6:12], AF.Ln, bias=epsb[:, 3:4], scale=1.0, acc=True)
                        tt("dve", bg[:, 6:12], bg[:, 6:12], prm[:, 0:6], ALU.mult, acc=True)
                        r0 = t0 + j * 128
                        dma(DV(gz[r0:r0 + 128, :]), zg[:], q="pool")
                        dma(DV(bgd[r0:r0 + 128, :]), bg[:], q="pool")
                phase_end()

        def tri_inverse(NG, Nm, NTm, Xm, Ym, T1s, T2s, pb, nlev):
            identg = V(ident_f.ap.unsqueeze(1).to_broadcast([128, NG, 128]), ident_f.bs)

            def mk(l, j):
                return V(msk.ap[:, 2 * l + j, :].unsqueeze(1).to_broadcast([128, NG, 128]), [msk])
            cp("pool", Xm[:], identg)
            cp("pool", Ym[:], identg)
            cpred(Xm[:], mk(0, 0), Nm[:])
            cpred(Ym[:], mk(0, 1), NTm[:])
            for l in range(1, nlev):
                last = (l == nlev - 1)
                p2 = pb.next()
                for g in range(NG):
                    mm(p2[:, g * 128:(g + 1) * 128], Nm[:, g, :], Ym[:, g, :])
                if not last:
                    p1 = pb.next()
                    for g in range(NG):
                        mm(p1[:, g * 128:(g + 1) * 128], NTm[:, g, :], Xm[:, g, :])
                    cp("act", T1s[:], p1.v(p1.ap[:, 0:NG * 128].rearrange("p (a b) -> p a b", a=NG)))
                cp("dve", T2s[:], p2.v(p2.ap[:, 0:NG * 128].rearrange("p (a b) -> p a b", a=NG)))
                p4 = pb.next()
                for g in range(NG):
                    mm(p4[:, g * 128:(g + 1) * 128], Xm[:, g, :], T2s[:, g, :])
                if not last:
                    p3 = pb.next()
                    for g in range(NG):
                        mm(p3[:, g * 128:(g + 1) * 128], Ym[:, g, :], T1s[:, g, :])
                    cpred(Xm[:], mk(l, 0), p3.v(p3.ap[:, 0:NG * 128].rearrange("p (a b) -> p a b", a=NG)))
                cpred(Ym[:], mk(l, 1), p4.v(p4.ap[:, 0:NG * 128].rearrange("p (a b) -> p a b", a=NG)))

        def g3(pbk, n=3):
            return pbk.v(pbk.ap[:, 0:n * 128].rearrange("p (a b) -> p a b", a=n))

        def g3b(pbk, n=3):
            return pbk.v(pbk.ap.bitcast(BF16)[:, 0:n * 128].rearrange("p (a b) -> p a b", a=n))

        def phase_gdn(e):
            with ExitStack() as ph:
                sb = mk_sb(ph)
                onr = sb([128, 128], F32, "onr")
                dma(onr[:], DV(I["ev_out_norm"][e:e + 1, :].partition_broadcast(128)))
                negus = sb([128, 128], F32, "negus")
                ts("dve", negus[:], ustr_f, -1.0, None, ALU.mult)
                S = [sb([128, 128], F32, "S") for _ in range(6)]
                Sb = [sb([128, 128], BF16, "Sb") for _ in range(6)]
                for h in range(6):
                    P.op("pool", lambda e_, h=h: e_.memset(S[h].ap, 0.0), writes=[S[h]])
                    P.op("pool", lambda e_, h=h: e_.memset(Sb[h].ap, 0.0), writes=[Sb[h]])
                qkl = RR([sb([128, 12, 128], BF16, "qk") for _ in range(2)])
                vl = RR([sb([128, 6, 128], BF16, "vT") for _ in range(2)])
                bgl = RR([sb([128, 12], F32, "bg") for _ in range(2)])
                gzl = RR([sb([128, 768], F32, "gz") for _ in range(2)])
                zggl = RR([sb([128, 768], F32, "zgg") for _ in range(2)])
                gbl = RR([sb([128, 12, 128], F32, "gb") for _ in range(2)])
                smal = RR([sb([128, 40], F32, "sma") for _ in range(2)])
                F3 = lambda nm, k=2: RR([sb([128, 3, 128], F32, nm) for _ in range(k)])
                B3 = lambda nm, k=2: RR([sb([128, 3, 128], BF16, nm) for _ in range(k)])
                Dml, Eml, EUl, ESl, egrl, t1l = F3("Dm"), F3("Em"), F3("EU"), F3("ES"), F3("egr"), F3("t1")
                NTl, Nl, Xl, Yl, T1l, T2l, usl, sql, onl = F3("NT"), F3("N"), F3("X"), F3("Y"), F3("T1"), F3("T2"), F3("us"), F3("sq"), F3("on")
                QKl, Tbl, kbgl, kdl, vbl, wTl, qdl, vnl, ytl, yTl = (B3("QK"), B3("Tb"), B3("kbg"), B3("kd"), B3("vb"), B3("wT"),
                                                                     B3("qd"), B3("vn"), B3("yt"), B3("yTs"))
                pb = RR(banks)
                for n in range(T // 128):
                    t0 = n * 128
                    qk = qkl.next()
                    vT_ = vl.next()
                    bg = bgl.next()
                    gzt = gzl.next()
                    dma(qk[:], DV(qkvT[0:12, :, t0:t0 + 128].rearrange("c p t -> p c t")))
                    dma(vT_[:], DV(qkvT[12:18, :, t0:t0 + 128].rearrange("c p t -> p c t")))
                    dma(bg[:], DV(bgd[t0:t0 + 128, :]))
                    dma(gzt[:], DV(gz[t0:t0 + 128, :]))
                    zgg = zggl.next()
                    tt("pool", zgg.v(zgg.ap.rearrange("p (a b) -> p a b", a=6)), gzt.v(gzt.ap.rearrange("p (a b) -> p a b", a=6)),
                       V(onr.ap.unsqueeze(1).to_broadcast([128, 6, 128]), [onr]), ALU.mult)
                    gb = gbl.next()
                    cp("dve", gb[:], V(bg.ap.unsqueeze(2).to_broadcast([128, 12, 128]), [bg]))
                    sma = smal.next()
                    pg = pb.next()
                    mm(pg[:, 0:6], utri_f, bg[:, 6:12])
                    cp("dve", sma[:, 0:6], pg[:, 0:6])
                    for grp in range(2):
                        hs = [grp * 3 + j for j in range(3)]
                        pA = pb.next()
                        pB = pb.next()
                        for j, h in enumerate(hs):
                            mm(pA[:, j * 128:(j + 1) * 128], gb[:, 6 + h, :], utri_f)
                            mm(pB[:, j * 128:(j + 1) * 128], gb[:, h, :], ident_f)
                        Dm, Em, EU, ES, egr = Dml.next(), Eml.next(), EUl.next(), ESl.next(), egrl.next()
                        for j, h in enumerate(hs):
                            ts("dve", Dm[:, j, :], pA[:, j * 128:(j + 1) * 128], sma[:, h:h + 1], 0.0, ALU.subtract, ALU.min,
                               acc=(j > 0))
                        act(Em[:], Dm[:], AF.Exp)
                        act(egr[:], g3(pA), AF.Exp)
                        cp("dve", sma[:, 6 + grp * 3:9 + grp * 3], pA.v(pA.ap[:, 0:384].rearrange("p (a b) -> p a b", a=3)[:, :, 127]),
                           acc=True)
                        tt("pool", EU[:], Em[:], V(utri_f.ap.unsqueeze(1).to_broadcast([128, 3, 128]), utri_f.bs), ALU.mult)
                        tt("pool", ES[:], Em[:], V(negus.ap.unsqueeze(1).to_broadcast([128, 3, 128]), [negus]), ALU.mult)
                        c6 = slice(grp * 3, grp * 3 + 3)
                        tt("dve", sma[:, 12 + grp * 3:15 + grp * 3], sma[:, 6 + grp * 3:9 + grp * 3], sma[:, grp * 3:grp * 3 + 3],
                           ALU.subtract, acc=True)
                        act(sma[:, 12 + grp * 3:15 + grp * 3], sma[:, 12 + grp * 3:15 + grp * 3], AF.Exp, acc=True)
                        act(sma[:, 18 + grp * 3:21 + grp * 3], sma[:, grp * 3:grp * 3 + 3], AF.Exp, acc=True)
                        act(sma[:, 24 + grp * 3:27 + grp * 3], sma[:, 6 + grp * 3:9 + grp * 3], AF.Exp, acc=True)
                        tt("dve", sma[:, 18 + grp * 3:21 + grp * 3], sma[:, 18 + grp * 3:21 + grp * 3], bg[:, grp * 3:grp * 3 + 3],
                           ALU.mult, acc=True)
                        pC = pb.next()
                        pD = pb.next()
                        for j, h in enumerate(hs):
                            mm(pC[:, j * 128:(j + 1) * 128], qk[:, 6 + h, :], qk[:, 6 + h, :])
                            mm(pD[:, j * 128:(j + 1) * 128], qk[:, 6 + h, :], qk[:, h, :])
                        t1, NT, QKT = t1l.next(), NTl.next(), QKl.next()
                        tt("dve", t1[:], g3(pC), ES[:], ALU.mult)
                        tt("dve", NT[:], g3(pB), t1[:], ALU.mult)
                        tt("dve", QKT[:], g3(pD), EU[:], ALU.mult)
                        pT = pb.next()
                        for j in range(3):
                            tr(pT[:, j * 128:(j + 1) * 128], NT[:, j, :], ident_f)
                        Nm = Nl.next()
                        cp("act", Nm[:], g3(pT))
                        Xm, Ym, T1s, T2s = Xl.next(), Yl.next(), T1l.next(), T2l.next()
                        tri_inverse(3, Nm, NT, Xm, Ym, T1s, T2s, pb, 7)
                        Tb = Tbl.next()
                        cp("pool", Tb[:], Ym[:])
                        pK = pb.next()
                        pV = pb.next()
                        for j, h in enumerate(hs):
                            tr(pK.v(pK.ap.bitcast(BF16)[:, j * 128:(j + 1) * 128]), qk[:, 6 + h, :], ident_b)
                            tr(pV.v(pV.ap.bitcast(BF16)[:, j * 128:(j + 1) * 128]), vT_[:, h, :], ident_b)
                        kbg, kd, vb = kbgl.next(), kdl.next(), vbl.next()
                        for j, h in enumerate(hs):
                            kt = pK.v(pK.ap.bitcast(BF16)[:, j * 128:(j + 1) * 128])
                            ts("dve", kbg[:, j, :], kt, sma[:, 18 + h:19 + h], None, ALU.mult, acc=(j > 0))
                            ts("dve", kd[:, j, :], kt, sma[:, 12 + h:13 + h], None, ALU.mult, acc=(j > 0))
                            ts("dve", vb[:, j, :], pV.v(pV.ap.bitcast(BF16)[:, j * 128:(j + 1) * 128]), bg[:, h:h + 1], None, ALU.mult,
                               acc=(j > 0))
                        pU = pb.next()
                        pW = pb.next()
                        for j in range(3):
                            mm(pU[:, j * 128:(j + 1) * 128], Tb[:, j, :], vb[:, j, :])
                            mm(pW[:, j * 128:(j + 1) * 128], kbg[:, j, :], Tb[:, j, :])
                        us, wT, qd = usl.next(), wTl.next(), qdl.next()
                        cp("act", us[:], g3(pU))
                        cp("act", wT[:], g3(pW))
                        tt("pool", qd[:], qk[:, grp * 3:grp * 3 + 3, :], egr[:], ALU.mult)
                        pS = pb.next()
                        for j, h in enumerate(hs):
                            mm(pS[:, j * 128:(j + 1) * 128], wT[:, j, :], Sb[h][:])
                        vn = vnl.next()
                        tt("dve", vn[:], us[:], g3(pS), ALU.subtract)
                        pO = pb.next()
                        for j, h in enumerate(hs):
                            mm(pO[:, j * 128:(j + 1) * 128], qd[:, j, :], Sb[h][:], start=True, stop=False)
                            mm(pO[:, j * 128:(j + 1) * 128], QKT[:, j, :], vn[:, j, :], start=False, stop=True)
                        pN = pb.next()
                        for j, h in enumerate(hs):
                            mm(pN[:, j * 128:(j + 1) * 128], kd[:, j, :], vn[:, j, :])
                        for j, h in enumerate(hs):
                            stt("dve", S[h][:], S[h][:], sma[:, 24 + h:25 + h], pN[:, j * 128:(j + 1) * 128], ALU.mult, ALU.add)
                            cp("act", Sb[h][:], S[h][:])
                        sq, on, yt = sql.next(), onl.next(), ytl.next()
                        act(sq[:], g3(pO), AF.Square)
                        red("dve", sma[:, 30:33], sq[:], ALU.add, acc=True)
                        act(sma[:, 33:36], sma[:, 30:33], AF.Sqrt, bias=eps6, scale=1.0 / 128, acc=True)
                        recip(sma[:, 33:36], sma[:, 33:36], acc=True)
                        tt("dve", on[:], g3(pO), V(sma.ap[:, 33:36].unsqueeze(2).to_broadcast([128, 3, 128]), [sma]), ALU.mult)
                        tt("pool", yt[:], on[:], zgg.v(zgg.ap.rearrange("p (a b) -> p a b", a=6)[:, grp * 3:grp * 3 + 3, :]), ALU.mult)
                        pY = pb.next()
                        for j in range(3):
                            tr(pY.v(pY.ap.bitcast(BF16)[:, j * 128:(j + 1) * 128]), yt[:, j, :], ident_b)
                        yTs = yTl.next()
                        cp("act", yTs[:], g3b(pY))
                        r0 = 256 + grp * 384
                        dma(DV(yT[r0:r0 + 384, t0:t0 + 128].rearrange("(j p) t -> p j t", p=128)), yTs[:], q="pool")
                phase_end()


        NFM = 7
        fmd = scr("fmd", [NFM, 512, T], F32)
        lwd = scr("lwd", [T, 512], F32)
        vfirst = scr("vfirst", [512, T], F32)
        csd = scr("csd", [T, 32], F32)
        qTm = scr("qTm", [8, 96, T], BF16)
        kTm = scr("kTm", [8, 96, T], BF16)
        vmd = scr("vmd", [8, 128, T // 128, 65], BF16)

        def phase_rope():
            with ExitStack() as ph:
                sb = mk_sb(ph)
                invf = sb([128, 16], F32, "invf")
                dma(invf[:], DV(I["ropec"][0:1, :].partition_broadcast(128)))
                TWO_PI = 2.0 * np.pi
                C1 = 6.28125
                C2 = TWO_PI - C1
                pil = RR([sb([128, 1], I32, "pi") for _ in range(2)])
                pfl = RR([sb([128, 1], F32, "pf") for _ in range(2)])
                angl = RR([sb([128, 32], F32, "ang") for _ in range(2)])
                kil = RR([sb([128, 32], I32, "ki") for _ in range(2)])
                kfl = RR([sb([128, 32], F32, "kf") for _ in range(2)])
                rl = RR([sb([128, 32], F32, "r") for _ in range(2)])
                ml = RR([sb([128, 32], F32, "m") for _ in range(2)])
                csl = RR([sb([128, 32], F32, "cs") for _ in range(2)])
                for t0 in range(0, T, 128):
                    pi_, pf, ang, ki, kf, r, m, cs = (pil.next(), pfl.next(), angl.next(), kil.next(), kfl.next(), rl.next(),
                                                      ml.next(), csl.next())
                    dma(pi_[:], DV(I["positions"][t0:t0 + 128, :]))
                    cp("dve", pf[:], pi_[:])
                    ts("dve", ang[:, 16:32], invf[:], pf[:, 0:1], None, ALU.mult)
                    ts("dve", ang[:, 0:16], ang[:, 16:32], np.pi / 2, None, ALU.add, acc=True)
                    ts("dve", kf[:], ang[:], 1.0 / TWO_PI, None, ALU.mult)
                    cp("dve", ki[:], kf[:])
                    cp("dve", kf[:], ki[:])
                    stt("dve", r[:], kf[:], -C1, ang[:], ALU.mult, ALU.add)
                    stt("dve", r[:], kf[:], -C2, r[:], ALU.mult, ALU.add)
                    ts("dve", m[:], r[:], np.pi, TWO_PI, ALU.is_gt, ALU.mult)
                    tt("dve", r[:], r[:], m[:], ALU.subtract)
                    ts("dve", m[:], r[:], -np.pi, TWO_PI, ALU.is_lt, ALU.mult)
                    tt("dve", r[:], r[:], m[:], ALU.add)
                    act(cs[:], r[:], AF.Sin)
                    dma(DV(csd[t0:t0 + 128, :]), cs[:], q="pool")
                phase_end()

        def phase_odin(o, L):
            with ExitStack() as ph:
                sb = mk_sb(ph)
                NCOL = 2592 + (32 if o == 1 else 0)
                win = sb([128, 8, NCOL], BF16, "win")
                wv = Wb["od_w_in"][o].rearrange("(k p) n -> p k n", p=128)
                for c in range(4):
                    dma(win[:, :, c * 648:(c + 1) * 648], DV(wv[:, :, c * 648:(c + 1) * 648]), acc=(c > 0))
                if o == 1:
                    dma(win[:, :, 2592:2624], DV(Wb["od_vres_w1"][0].rearrange("(k p) n -> p k n", p=128)), acc=True)
                wuq = sb([128, 4, 768], BF16, "wuq")
                dma(wuq[:], DV(Wb["od_w_uq"][o].rearrange("(k p) n -> p k n", p=128)))
                wukv = sb([128, 2, 1024], BF16, "wukv")
                dma(wukv[:], DV(Wb["od_w_ukv"][o].rearrange("(k p) n -> p k n", p=128)))
                lw2 = sb([128, 512], BF16, "lw2")
                dma(lw2[0:64, :], DV(Wb["od_w2"][o]))
                dma(lw2[64:128, :], DV(Wb["od_a2"][o]), acc=True)
                g2b = sb([128, 512], BF16, "g2b")
                dma(g2b[:], DV(Wb["od_g2"][o]))
                gain = sb([128, D], F32, "gain")
                dma(gain[:], DV(I["norm_mix"][L:L + 1, :].partition_broadcast(128)))
                rows = sb([128, 512 + 512 + 256 + 96 + 96], F32, "rows")
                dma(rows[:, 0:512], DV(I["od_w0"][o:o + 1, :].partition_broadcast(128)))
                dma(rows[:, 512:1024], DV(I["od_qa_norm"][o:o + 1, :].partition_broadcast(128)), acc=True)
                dma(rows[:, 1024:1280], DV(I["od_kva_norm"][o:o + 1, :].partition_broadcast(128)), acc=True)
                dma(rows[:, 1280:1376], DV(I["od_q_ln"][o:o + 1, :].partition_broadcast(128)), acc=True)
                dma(rows[:, 1376:1472], DV(I["od_k_ln"][o:o + 1, :].partition_broadcast(128)), acc=True)
                pp = sb([128, 40], F32, "pp")
                P.op("pool", lambda e_: e_.memset(pp.ap, 0.0), writes=[pp])
                dma(pp[:, 0:14], DV(I["od_shift_mu"][o].rearrange("(c p) -> p c", p=128)), slow=True, acc=True)
                dma(pp[:, 16:20], DV(I["od_a0"][o].rearrange("(c p) -> p c", p=128)), slow=True, acc=True)
                dma(pp[:, 20:24], DV(I["od_k_k"][o].rearrange("(c p) -> p c", p=128)), slow=True, acc=True)
                dma(pp[:, 24:28], DV(I["od_k_a"][o].rearrange("(c p) -> p c", p=128)), slow=True, acc=True)
                dma(pp[:, 28:32], DV(I["od_r_k"][o].rearrange("(c h) k -> (h k) c", h=2)), slow=True, acc=True)
                if o == 1:
                    dma(pp[0:32, 14:15], DV(I["od_vres_mu"][0].rearrange("(c p) -> p c", p=32)), slow=True, acc=True)
                    dma(pp[:, 32:36], DV(I["od_vres_v0"][0].rearrange("(c p) -> p c", p=128)), slow=True, acc=True)
                    v2b = sb([32, 512], BF16, "v2b")
                    dma(v2b[:], DV(Wb["od_vres_v2"][0]))
                NZ = 15 if o == 1 else 14
                zs = sb([128, 15, 257], F32, "zs")
                P.op("pool", lambda e_: e_.memset(zs.ap[:, :, 0:1], 0.0), writes=[zs])
                zl = sb([128, 15, 256], F32, "zl")
                xbl = RR([sb([128, D], F32, "xb") for _ in range(3)])
                xnl = RR([sb([128, D], BF16, "xn") for _ in range(2)])
                xnTl = RR([sb([128, 8, 256], BF16, "xnT") for _ in range(2)])
                junk = sb([128, D], F32, "junk")
                ssl = RR([sb([128, 2], F32, "ss") for _ in range(6)])
                F2 = lambda nm, k=2: RR([sb([128, 256], F32, nm) for _ in range(k)])
                dl, al, t1l, t2l, sql, rnl, kkl, kfl, bbl, bvl, vfl, gl_ = (F2("d", 3), F2("a"), F2("t1"), F2("t2"), F2("sq"), F2("rn"),
                                                                           F2("kk"), F2("kf"), F2("bb"), F2("bv"), F2("vf"), F2("g"))
                twl = RR([sb([128, 256], BF16, "tw") for _ in range(2)])
                sgl = RR([sb([128, 256], BF16, "sgd") for _ in range(2)])
                vdl = RR([sb([32, 256], BF16, "vd") for _ in range(2)])
                lwl = RR([sb([128, 512], F32, "lw") for _ in range(2)])
                cql = RR([sb([128, 768], BF16, "cqn") for _ in range(2)])
                cTl = RR([sb([128, 6, 128], BF16, "cT") for _ in range(2)])
                qfl = RR([sb([128, 8, 96], F32, "qf") for _ in range(1)])
                kfl2 = RR([sb([128, 8, 96], F32, "kf2") for _ in range(1)])
                qbl = RR([sb([128, 8, 96], BF16, "qb") for _ in range(1)])
                kbl = RR([sb([128, 8, 96], BF16, "kb") for _ in range(1)])
                vel = RR([sb([128, 8, 65], BF16, "ve") for _ in range(2)])
                s8l = RR([sb([128, 40], F32, "s8") for _ in range(2)])
                sq8l = RR([sb([128, 8, 96], F32, "sq8") for _ in range(1)])
                krl = RR([sb([128, 32], F32, "kr") for _ in range(2)])
                csl = RR([sb([128, 32], F32, "cs") for _ in range(2)])
                rtl = RR([sb([128, 8, 16], F32, "rt") for _ in range(4)])
                qTl = RR([sb([96, 8, 128], BF16, "qTs") for _ in range(2)])
                kTl = RR([sb([96, 8, 128], BF16, "kTs") for _ in range(2)])
                ptr = RR(banks[0:2])
                pfm = RR(banks[2:5])
                ptk = RR(banks[5:8])
                E05 = float(np.exp(-0.5))

                def rope(xv, cs):
                    cb = V(cs.ap[:, 0:16].unsqueeze(1).to_broadcast([128, 8, 16]), [cs])
                    sb_ = V(cs.ap[:, 16:32].unsqueeze(1).to_broadcast([128, 8, 16]), [cs])
                    x1 = V(xv.ap[:, :, 0:16], xv.bs)
                    x2 = V(xv.ap[:, :, 16:32], xv.bs)
                    a1, a2, a3, a4 = rtl.next(), rtl.next(), rtl.next(), rtl.next()
                    tt("dve", a1[:], x1, cb, ALU.mult)
                    tt("pool", a2[:], x2, sb_, ALU.mult)
                    tt("dve", a3[:], x2, cb, ALU.mult)
                    tt("pool", a4[:], x1, sb_, ALU.mult)
                    tt("dve", x1, a1[:], a2[:], ALU.subtract, acc=True)
                    tt("dve", x2, a3[:], a4[:], ALU.add, acc=True)

                for t0 in range(0, T, 256):
                    xnT = xnTl.next()
                    for j in range(2):
                        xb = xbl.next()
                        dma(xb[:], DV(xs[t0 + j * 128:t0 + (j + 1) * 128, :]))
                        norm_transpose(xb, gain, xnl.next(), xnT, j * 128, (junk, ssl.next()), ptr, j == 0)
                    colmap = list(range(14)) + ([None] if o == 1 else [])
                    for c in range(NZ):
                        pf = pfm.next()
                        if c < 14:
                            for k in range(8):
                                mm(pf[:, 0:256], win[:, k, c * 128:(c + 1) * 128], xnT[:, k, :], start=(k == 0), stop=(k == 7))
                            evac(zs[:, c, 1:257], pf[:, 0:256], acc=True)
                        else:
                            for k in range(8):
                                mm(pf[0:32, 0:256], win[:, k, 2592:2624], xnT[:, k, :], start=(k == 0), stop=(k == 7))
                            evac(zs[0:32, c, 1:257], pf[0:32, 0:256], acc=True)
                    for c in range(NZ):
                        np_ = 128 if c < 14 else 32
                        d = dl.next()
                        tt("pool", d[0:np_, :], zs[0:np_, c, 0:256], zs[0:np_, c, 1:257], ALU.subtract)
                        stt("dve", zl[0:np_, c, :], d[0:np_, :], pp[0:np_, c:c + 1], zs[0:np_, c, 1:257], ALU.mult, ALU.add, acc=True)
                    cp("pool", zs[:, :, 0:1], zs[:, :, 256:257], acc=True)
                    tw = twl.next()
                    act(tw[0:64, :], zl[0:64, 12, :], AF.Tanh)
                    cp("pool", tw[64:128, :], zl[64:128, 12, :], acc=True)
                    sgd = sgl.next()
                    act(sgd[:], zl[:, 13, :], AF.Sigmoid)
                    if o == 1:
                        vd = vdl.next()
                        cp("pool", vd[:], zl[0:32, 14, :])
                    for j in range(2):
                        pw = ptk.next()
                        mm(pw[:, :], tw[0:64, j * 128:(j + 1) * 128], lw2[0:64, :])
                        lw = lwl.next()
                        tt("dve", lw[:], pw[:, :], rows[:, 0:512], ALU.add)
                        act(lw[:], lw[:], AF.Sigmoid)
                        ts("pool", lw[:], lw[:], -E05, None, ALU.mult)
                        dma(DV(lwd[t0 + j * 128:t0 + (j + 1) * 128, :]), lw[:], q="pool")
                    for c in range(4):
                        rr_, kr_, vr_ = zl[:, c, :], zl[:, 4 + c, :], zl[:, 8 + c, :]
                        pa = pfm.next()
                        mm(pa[:, 0:256], lw2[64:128, c * 128:(c + 1) * 128], tw[64:128, :])
                        a_ = al.next()
                        act(a_[:], pa[:, 0:256], AF.Sigmoid, bias=pp[:, 16 + c:17 + c], scale=1.0)
                        pg = pfm.next()
                        mm(pg[:, 0:256], g2b[:, c * 128:(c + 1) * 128], sgd[:])
                        g_ = gl_.next()
                        cp("act", g_[:], pg[:, 0:256])
                        dma(DV(fmd[6, c * 128:(c + 1) * 128, t0:t0 + 256]), g_[:], q="pool")
                        if o == 1:
                            pv = pfm.next()
                            mm(pv[:, 0:256], v2b[0:32, c * 128:(c + 1) * 128], vd[0:32, :])
                            t1 = t1l.next()
                            act(t1[:], pv[:, 0:256], AF.Sigmoid, bias=pp[:, 32 + c:33 + c], scale=1.0)
                            vf = vfl.next()
                            dma(vf[:], DV(vfirst[c * 128:(c + 1) * 128, t0:t0 + 256]))
                            tt("dve", vf[:], vf[:], vr_, ALU.subtract)
                            tt("dve", vf[:], vf[:], t1[:], ALU.mult)
                            tt("dve", vr_, vr_, vf[:], ALU.add, acc=True)
                        else:
                            dma(DV(vfirst[c * 128:(c + 1) * 128, t0:t0 + 256]), vr_, q="pool")
                        dma(DV(fmd[2, c * 128:(c + 1) * 128, t0:t0 + 256]), vr_, q="pool")
                        dma(DV(fmd[0, c * 128:(c + 1) * 128, t0:t0 + 256]), rr_, q="pool")
                        kk = kkl.next()
                        ts("dve", kk[:], kr_, pp[:, 20 + c:21 + c], None, ALU.mult)
                        sq = sql.next()
                        tt("pool", sq[:], kk[:], kk[:], ALU.mult)
                        pq = pfm.next()
                        mm(pq[:, 0:256], cst[:, 4, :], sq[:])
                        rn = rnl.next()
                        act(rn[:], pq[:, 0:256], AF.Sqrt, bias=eps6, scale=1.0)
                        recip(rn[:], rn[:])
                        tt("dve", kk[:], kk[:], rn[:], ALU.mult)
                        dma(DV(fmd[3, c * 128:(c + 1) * 128, t0:t0 + 256]), kk[:], q="pool")
                        t2 = t2l.next()
                        ts("dve", t2[:], a_[:], -1.0, pp[:, 24 + c:25 + c], ALU.add, ALU.mult)
                        kf = kfl.next()
                        stt("dve", kf[:], t2[:], 1.0, kr_, ALU.add, ALU.mult)
                        dma(DV(fmd[1, c * 128:(c + 1) * 128, t0:t0 + 256]), kf[:], q="pool")
                        bb = bbl.next()
                        tt("pool", bb[:], kk[:], a_[:], ALU.mult)
                        dma(DV(fmd[4, c * 128:(c + 1) * 128, t0:t0 + 256]), bb[:], q="pool")
                        sq2 = sql.next()
                        stt("dve", sq2[:], rr_, pp[:, 28 + c:29 + c], kf[:], ALU.mult, ALU.mult)
                        pb_ = pfm.next()
                        mm(pb_[:, 0:256], cst[:, 4, :], sq2[:])
                        bv = bvl.next()
                        tt("dve", bv[:], pb_[:, 0:256], vr_, ALU.mult)
                        dma(DV(fmd[5, c * 128:(c + 1) * 128, t0:t0 + 256]), bv[:], q="pool")
                    for j in range(2):
                        r0 = t0 + j * 128
                        p1 = ptk.next()
                        p2 = ptk.next()
                        for k in range(8):
                            mm(p1[:, :], xnT[:, k, j * 128:(j + 1) * 128], win[:, k, 1792:2304], start=(k == 0), stop=(k == 7))
                        for k in range(8):
                            mm(p2[:, 0:288], xnT[:, k, j * 128:(j + 1) * 128], win[:, k, 2304:2592], start=(k == 0), stop=(k == 7))
                        cqn = cql.next()
                        rmsnorm_tile((junk, ssl.next()), p1[:, :], rows[:, 512:1024], cqn[:, 0:512], 512)
                        rmsnorm_tile((junk, ssl.next()), p2[:, 0:256], rows[:, 1024:1280], cqn[:, 512:768], 256)
                        kr = krl.next()
                        cp("act", kr[:], p2[:, 256:288])
                        cs = csl.next()
                        dma(cs[:], DV(csd[r0:r0 + 128, :]))
                        cT = cTl.next()
                        for half in range(2):
                            pt = ptr.next()
                            for i3 in range(3):
                                kc = half * 3 + i3
                                tr(pt.v(pt.ap.bitcast(BF16)[:, i3 * 128:(i3 + 1) * 128]), cqn[:, kc * 128:(kc + 1) * 128], ident_b)
                            evac(cT[:, half * 3:half * 3 + 3, :], g3b(pt), acc=(half > 0))
                        pq1 = ptk.next()
                        pq2 = ptk.next()
                        for k in range(4):
                            mm(pq1[:, :], cT[:, k, :], wuq[:, k, 0:512], start=(k == 0), stop=(k == 3))
                        for k in range(4):
                            mm(pq2[:, 0:256], cT[:, k, :], wuq[:, k, 512:768], start=(k == 0), stop=(k == 3))
                        qf = qfl.next()
                        qfl_ = qf.v(qf.ap.rearrange("p h d -> p (h d)"))
                        cp("act", V(qfl_.ap[:, 0:512], qf.bs), pq1[:, :])
                        cp("dve", V(qfl_.ap[:, 512:768], qf.bs), pq2[:, 0:256], acc=True)
                        pk1 = ptk.next()
                        pk2 = ptk.next()
                        for k in range(2):
                            mm(pk1[:, :], cT[:, 4 + k, :], wukv[:, k, 0:512], start=(k == 0), stop=(k == 1))
                        for k in range(2):
                            mm(pk2[:, :], cT[:, 4 + k, :], wukv[:, k, 512:1024], start=(k == 0), stop=(k == 1))
                        kf2 = kfl2.next()
                        ve = vel.next()
                        for hb, pk in enumerate((pk1, pk2)):
                            pkv = pk.v(pk.ap.rearrange("p (h d) -> p h d", h=4))
                            cp("act", kf2[:, hb * 4:hb * 4 + 4, 0:64], V(pkv.ap[:, :, 0:64], pk.bs), acc=(hb > 0))
                            cp("dve", ve[:, hb * 4:hb * 4 + 4, 0:64], V(pkv.ap[:, :, 64:128], pk.bs), acc=(hb > 0))
                        P.op("pool", lambda e_, ve=ve: e_.memset(ve.ap[:, :, 64:65], 1.0), accs=[ve])
                        cp("pool", kf2[:, :, 64:96], V(kr.ap.unsqueeze(1).to_broadcast([128, 8, 32]), [kr]), acc=True)
                        dma(DV(vmd[:, :, r0 // 128, :].rearrange("h p d -> p h d")), ve[:], q="pool")
                        for (xf, gcol, xb_, dst, Tl) in ((qf, 1280, qbl, qTm, qTl), (kf2, 1376, kbl, kTm, kTl)):
                            s8 = s8l.next()
                            sq8 = sq8l.next()
                            tt("pool", sq8[:], xf[:], xf[:], ALU.mult)
                            red("dve", s8[:, 0:8], sq8[:], ALU.add)
                            act(s8[:, 8:16], s8[:, 0:8], AF.Sqrt, bias=eps6, scale=1.0 / 96, acc=True)
                            recip(s8[:, 8:16], s8[:, 8:16], acc=True)
                            tt("dve", xf[:], xf[:], V(s8.ap[:, 8:16].unsqueeze(2).to_broadcast([128, 8, 96]), [s8]), ALU.mult)
                            tt("pool", xf[:], xf[:], V(rows.ap[:, gcol:gcol + 96].unsqueeze(1).to_broadcast([128, 8, 96]), [rows]),
                               ALU.mult)
                            rope(V(xf.ap[:, :, 64:96], xf.bs), cs)
                            xb2 = xb_.next()
                            cp("pool", xb2[:], xf[:])
                            xT = Tl.next()
                            for half in range(2):
                                pt = ptr.next()
                                for i4 in range(4):
                                    h = half * 4 + i4
                                    tr(pt.v(pt.ap.bitcast(BF16)[0:96, i4 * 128:(i4 + 1) * 128]), xb2[:, h, :], ident_b)
                                evac(xT[:, half * 4:half * 4 + 4, :],
                                     pt.v(pt.ap.bitcast(BF16)[0:96, 0:512].rearrange("p (a b) -> p a b", a=4)), acc=(half > 0))
                            dma(DV(dst[:, :, r0:r0 + 128].rearrange("h d t -> d h t")), xT[:], q="pool")
                phase_end()


        def phase_rwkv(o):
            with ExitStack() as ph:
                sb = mk_sb(ph)
                lnw = sb([128, 1024], F32, "lnw")
                dma(lnw[:, 0:512], DV(I["od_lnx_w"][o:o + 1, :].partition_broadcast(128)))
                dma(lnw[:, 512:1024], DV(I["od_lnx_b"][o:o + 1, :].partition_broadcast(128)), acc=True)
                H = [sb([128, 64], F32, "H") for _ in range(4)]
                Hb = [sb([128, 64], BF16, "Hb") for _ in range(4)]
                for c in range(4):
                    P.op("pool", lambda e_, c=c: e_.memset(H[c].ap, 0.0), writes=[H[c]])
                    P.op("pool", lambda e_, c=c: e_.memset(Hb[c].ap, 0.0), writes=[Hb[c]])
                fml = RR([sb([128, 7, 4, 128], F32, "fm") for _ in range(2)])
                lwl = RR([sb([128, 512], F32, "lw") for _ in range(2)])
                F1 = lambda nm, k=2: RR([sb([128, 128], F32, nm) for _ in range(k)])
                B1 = lambda nm, k=2: RR([sb([128, 128], BF16, nm) for _ in range(k)])
                Epl, Eml, Exl, Edl = F1("Ep"), F1("Em"), F1("Ex"), F1("Ed")
                ccl = RR([sb([128, 4], F32, "cc") for _ in range(4)])
                arl = RR([sb([128, 2, 128], BF16, "ar") for _ in range(5)])
                btl, ktl, bdl, kdl = B1("bt", 5), B1("kt", 5), B1("bd"), B1("kd")
                vtl, bdtl, kdtl = B1("vt", 5), B1("bdt", 5), B1("kdt", 5)
                F4 = lambda nm, k=2: RR([sb([128, 4, 128], F32, nm) for _ in range(k)])
                B4 = lambda nm, k=2: RR([sb([128, 4, 128], BF16, nm) for _ in range(k)])
                NTl, Nl, Xl, Yl, T1l, T2l = F4("NT"), F4("N"), F4("X"), F4("Y"), F4("T1"), F4("T2")
                AKl, RBl, RKl, Tbl = B4("AK"), B4("RB"), B4("RK"), B4("Tb")
                inl = RR([sb([128, 4, 64], BF16, "inn") for _ in range(2)])
                zl_ = RR([sb([128, 4, 64], BF16, "Zs") for _ in range(2)])
                yol = RR([sb([128, 8, 64], F32, "yo") for _ in range(2)])
                sq8 = sb([128, 8, 64], F32, "sq8")
                stl = RR([sb([128, 40], F32, "st") for _ in range(2)])
                youtl = RR([sb([128, 128], BF16, "yout") for _ in range(3)])
                t5l = F1("t5", 3)
                pb = RR(banks)
                for n in range(T // 128):
                    t0 = n * 128
                    fm = fml.next()
                    dma(fm.v(fm.ap.rearrange("p a c t -> p (a c) t")),
                        DV(fmd[:, :, t0:t0 + 128].rearrange("a (c p) t -> p (a c) t", p=128)))
                    lw = lwl.next()
                    dma(lw[:], DV(lwd[t0:t0 + 128, :]))
                    R_, K_, V_, KK_, B_, BV_, G_ = [fm.v(fm.ap[:, a]) for a in range(7)]
                    pairs = []
                    for c in range(4):
                        pcl = pb.next()
                        mm(pcl[:, 0:128], lw[:, c * 128:(c + 1) * 128], utri_f)
                        mm(pcl[:, 128:256], lw[:, c * 128:(c + 1) * 128], ustr_f)
                        Ep, Em, Ex, Ed, cc = Epl.next(), Eml.next(), Exl.next(), Edl.next(), ccl.next()
                        act(Ep[:], pcl[:, 0:128], AF.Exp)
                        act(Em[:], pcl[:, 0:128], AF.Exp, scale=-1.0)
                        act(Ex[:], pcl[:, 128:256], AF.Exp)
                        cp("dve", cc[:, 0:1], pcl[:, 127:128])
                        act(Ed[:], pcl[:, 0:128], AF.Exp, bias=cc[:, 0:1], scale=-1.0)
                        act(cc[:, 1:2], cc[:, 0:1], AF.Exp, acc=True)
                        ar, bt, kt, bd, kd = arl.next(), btl.next(), ktl.next(), bdl.next(), kdl.next()
                        sl = lambda X_: V(X_.ap[:, c, :], X_.bs)
                        stt("dve", ar[:, 0, :], sl(KK_), -1.0, Ex[:], ALU.mult, ALU.mult)
                        tt("pool", ar[:, 1, :], sl(R_), Ep[:], ALU.mult, acc=True)
                        tt("dve", bt[:], sl(B_), Em[:], ALU.mult)
                        tt("pool", kt[:], sl(K_), Em[:], ALU.mult)
                        tt("dve", bd[:], sl(B_), Ed[:], ALU.mult)
                        tt("pool", kd[:], sl(K_), Ed[:], ALU.mult)
                        pt1 = pb.next()
                        tr(pt1[:, 0:128], sl(V_), ident_f)
                        vt = vtl.next()
                        cp("act", vt[:], pt1[:, 0:128])
                        pt2 = pb.next()
                        tr(pt2.v(pt2.ap.bitcast(BF16)[:, 0:128]), bd[:], ident_b)
                        tr(pt2.v(pt2.ap.bitcast(BF16)[:, 128:256]), kd[:], ident_b)
                        bdt, kdt = bdtl.next(), kdtl.next()
                        cp("act", bdt[:], pt2.v(pt2.ap.bitcast(BF16)[:, 0:128]))
                        cp("dve", kdt[:], pt2.v(pt2.ap.bitcast(BF16)[:, 128:256]))
                        pairs.append((ar, bt, kt, vt, bdt, kdt, cc))
                    yo = yol.next()
                    for grp in range(2):
                        NT, AK, RB, RK = NTl.next(), AKl.next(), RBl.next(), RKl.next()
                        for hl in range(4):
                            c = grp * 2 + hl // 2
                            hs = slice((hl % 2) * 64, (hl % 2) * 64 + 64)
                            ar, bt, kt, vt, bdt, kdt, cc = pairs[c]
                            pm1 = pb.next()
                            mm(pm1[:, 0:256], bt[hs, :], ar.v(ar.ap[hs, :, :].rearrange("p a t -> p (a t)")))
                            mm(pm1[:, 256:512], kt[hs, :], ar.v(ar.ap[hs, :, :].rearrange("p a t -> p (a t)")))
                            tt("dve", NT[:, hl, :], pm1[:, 0:128], ustr_f, ALU.mult, acc=(hl > 0))
                            tt("dve", RB[:, hl, :], pm1[:, 128:256], utri_f, ALU.mult, acc=(hl > 0))
                            tt("dve", AK[:, hl, :], pm1[:, 256:384], ustr_f, ALU.mult, acc=(hl > 0))
                            tt("dve", RK[:, hl, :], pm1[:, 384:512], utri_f, ALU.mult, acc=(hl > 0))
                        pT = pb.next()
                        for hl in range(4):
                            tr(pT[:, hl * 128:(hl + 1) * 128], NT[:, hl, :], ident_f)
                        Nm = Nl.next()
                        cp("act", Nm[:], g3(pT, 4))
                        Xm, Ym, T1s, T2s = Xl.next(), Yl.next(), T1l.next(), T2l.next()
                        tri_inverse(4, Nm, NT, Xm, Ym, T1s, T2s, pb, 7)
                        Tb = Tbl.next()
                        cp("pool", Tb[:], Ym[:])
                        pI = pb.next()
                        for hl in range(4):
                            c = grp * 2 + hl // 2
                            hh = hl % 2
                            hs = slice(hh * 64, hh * 64 + 64)
                            ar, bt, kt, vt, bdt, kdt, cc = pairs[c]
                            mm(pI[:, hl * 64:(hl + 1) * 64], ar[hs, 0, :], Hb[c][hs, :], start=True, stop=False)
                            mm(pI[:, hl * 64:(hl + 1) * 64], AK[:, hl, :], vt[:, hh * 64:(hh + 1) * 64], start=False, stop=True)
                        inn = inl.next()
                        cp("act", inn[:], pI.v(pI.ap[:, 0:256].rearrange("p (a b) -> p a b", a=4)))
                        pZ = pb.next()
                        for hl in range(4):
                            mm(pZ[:, hl * 64:(hl + 1) * 64], Tb[:, hl, :], inn[:, hl, :])
                        Zs = zl_.next()
                        cp("dve", Zs[:], pZ.v(pZ.ap[:, 0:256].rearrange("p (a b) -> p a b", a=4)))
                        pY = pb.next()
                        for hl in range(4):
                            c = grp * 2 + hl // 2
                            hh = hl % 2
                            hs = slice(hh * 64, hh * 64 + 64)
                            ar, bt, kt, vt, bdt, kdt, cc = pairs[c]
                            mm(pY[:, hl * 64:(hl + 1) * 64], ar[hs, 1, :], Hb[c][hs, :], start=True, stop=False)
                            mm(pY[:, hl * 64:(hl + 1) * 64], RB[:, hl, :], Zs[:, hl, :], start=False, stop=False)
                            mm(pY[:, hl * 64:(hl + 1) * 64], RK[:, hl, :], vt[:, hh * 64:(hh + 1) * 64], start=False, stop=True)
                        cp("act", yo[:, grp * 4:grp * 4 + 4, :], pY.v(pY.ap[:, 0:256].rearrange("p (a b) -> p a b", a=4)), acc=(grp > 0))
                        for cl_ in range(2):
                            c = grp * 2 + cl_
                            ar, bt, kt, vt, bdt, kdt, cc = pairs[c]
                            pH = pb.next()
                            mm(pH[:, 0:128], bdt[:], Zs.v(Zs.ap[:, cl_ * 2:cl_ * 2 + 2, :].rearrange("p a b -> p (a b)")),
                               start=True, stop=False)
                            mm(pH[:, 0:128], kdt[:], vt[:], start=False, stop=True)
                            for hh in range(2):
                                hs = slice(hh * 64, hh * 64 + 64)
                                stt("dve", H[c][hs, :], H[c][hs, :], cc[hs, 1:2], pH[hs, hh * 64:(hh + 1) * 64], ALU.mult, ALU.add,
                                    acc=(hh > 0))
                            cp("act", Hb[c][:], H[c][:])
                    st = stl.next()
                    red("dve", st[:, 0:8], yo[:], ALU.add)
                    tt("pool", sq8[:], yo[:], yo[:], ALU.mult)
                    red("dve", st[:, 8:16], sq8[:], ALU.add, acc=True)
                    ts("dve", st[:, 0:8], st[:, 0:8], 1.0 / 64, None, ALU.mult, acc=True)
                    tt("dve", st[:, 16:24], st[:, 0:8], st[:, 0:8], ALU.mult, acc=True)
                    stt("dve", st[:, 8:16], st[:, 8:16], 1.0 / 64, st[:, 16:24], ALU.mult, ALU.subtract, acc=True)
                    act(st[:, 8:16], st[:, 8:16], AF.Sqrt, bias=epsb[:, 1:2], scale=1.0, acc=True)
                    recip(st[:, 8:16], st[:, 8:16], acc=True)
                    tt("dve", yo[:], yo[:], V(st.ap[:, 0:8].unsqueeze(2).to_broadcast([128, 8, 64]), [st]), ALU.subtract)
                    tt("dve", yo[:], yo[:], V(st.ap[:, 8:16].unsqueeze(2).to_broadcast([128, 8, 64]), [st]), ALU.mult)
                    yof = yo.v(yo.ap.rearrange("p h v -> p (h v)"))
                    tt("pool", yof, yof, lnw[:, 0:512], ALU.mult)
                    tt("pool", yof, yof, lnw[:, 512:1024], ALU.add)
                    for c in range(4):
                        pO = pb.next()
                        tr(pO[:, 0:128], V(yof.ap[:, c * 128:(c + 1) * 128], yo.bs), ident_f)
                        t5 = t5l.next()
                        tt("dve", t5[:], pO[:, 0:128], V(BV_.ap[:, c, :], fm.bs), ALU.add)
                        yout = youtl.next()
                        tt("pool", yout[:], t5[:], V(G_.ap[:, c, :], fm.bs), ALU.mult)
                        dma(DV(yT[c * 128:(c + 1) * 128, t0:t0 + 128]), yout[:], q="pool")
                phase_end()

        def phase_mla(o):
            with ExitStack() as ph:
                sb = mk_sb(ph)
                QB = 512
                qhl = RR([sb([96, T], BF16, "qh") for _ in range(2)])
                khl = RR([sb([96, T], BF16, "kh") for _ in range(2)])
                vhl = RR([sb([128, T // 128, 65], BF16, "vh") for _ in range(2)])
                pexl = RR([sb([128, QB], BF16, "pex") for _ in range(4)])
                rcl = RR([sb([65, QB], F32, "rc") for _ in range(2)])
                ol = RR([sb([64, QB], BF16, "o") for _ in range(3)])
                one1 = sb([65, 64], F32, "one1")
                P.op("pool", lambda e_: e_.memset(one1.ap, 1.0), writes=[one1])
                ut_b = sb([128, 128], BF16, "utb")
                cp("dve", ut_b[:], utri_f)
                psc = RR(banks[0:4])
                pov = RR(banks[4:6])
                pbc = RR(banks[6:8])
                SC = 96.0 ** -0.5
                for h in range(8):
                    qh, kh, vh = qhl.next(), khl.next(), vhl.next()
                    dma(qh[:], DV(qTm[h]))
                    dma(kh[:], DV(kTm[h]))
                    dma(vh[:], DV(vmd[h]))
                    for q0 in range(0, T, QB):
                        po = pov.next()
                        nkb = (q0 + QB) // 128
                        for kb in range(nkb):
                            k0 = kb * 128
                            off = max(0, k0 - q0)
                            ps_ = psc.next()
                            mm(ps_[:, off:QB], kh[:, k0:k0 + 128], qh[:, q0 + off:q0 + QB])
                            pex = pexl.next()
                            act(pex[:, off:QB], ps_[:, off:QB], AF.Exp, scale=SC)
                            if k0 >= q0:
                                tt("pool", pex[:, off:off + 128], pex[:, off:off + 128], ut_b[:], ALU.mult, acc=True)
                            mm(po[0:65, off:QB], vh[:, kb, :], pex[:, off:QB], start=(kb == 0), stop=(kb == nkb - 1))
                        rc = rcl.next()
                        recip(rc[64:65, :], po[64:65, :])
                        pbk = pbc.next()
                        mm(pbk[0:64, :], one1[64:65, :], rc[64:65, :])
                        ob = ol.next()
                        ocp = sb
                        t_o = ol.next()
                        cp("act", t_o[:], po[0:64, :])
                        tt("dve", ob[:], t_o[:], pbk[0:64, :], ALU.mult)
                        r0 = 512 + h * 64
                        dma(DV(yT[r0:r0 + 64, q0:q0 + QB]), ob[:], q="pool")
                phase_end()

        PH = dict(outproj=phase_outproj, ffn=phase_ffn, evin=phase_evin, gdn=phase_gdn, rope=phase_rope, odin=phase_odin, rwkv=phase_rwkv, mla=phase_mla)
        for step in plan:
            PH[step[0]](*step[1:])
        P.barrier()
        P.emit()
        print("ops", P.nops)
    return nc, list(I.keys())


FULL_PLAN = [
    ("rope",),
    ("evin", 0, 0), ("gdn", 0), ("outproj", "ev_w_out", 0), ("ffn", 0, False),
    ("odin", 0, 1), ("rwkv", 0), ("mla", 0), ("outproj", "od_w_out", 0), ("ffn", 1, False),
    ("evin", 1, 2), ("gdn", 1), ("outproj", "ev_w_out", 1), ("ffn", 2, False),
    ("odin", 1, 3), ("rwkv", 1), ("mla", 1), ("outproj", "od_w_out", 1), ("ffn", 3, True),
]
N_CORES = 4


def kernel(**inputs):
    x = np.asarray(inputs["x"], dtype=np.float32)
    B, T, _ = x.shape
    nc, used = build(T, FULL_PLAN)
    pos = np.asarray(inputs["positions"]).astype(np.int32)
    consts = make_consts()
    masks = make_masks()
    ropec = make_ropec()
    in_maps = []
    for b in range(N_CORES):
        m = {}
        for k in used:
            if k == "x":
                m[k] = np.ascontiguousarray(x[b])
            elif k == "positions":
                m[k] = np.ascontiguousarray(pos[b].reshape(T, 1))
            elif k == "consts":
                m[k] = consts
            elif k == "masks":
                m[k] = masks
            elif k == "ropec":
                m[k] = ropec
            else:
                m[k] = np.ascontiguousarray(np.asarray(inputs[k], dtype=np.float32))
        in_maps.append(m)
    res = run_bass_kernel_spmd(nc, in_maps, core_ids=list(range(N_CORES)))
    out = np.stack([np.asarray(res.results[b]["out"], dtype=np.float32) for b in range(B)], 0)
    return out
```
